# Optimizing a Trainium2 kernel written in Bass

```python
import jax, jax.numpy as jnp
from jax import lax
import numpy as np

D_MODEL = 1024
BATCH = 8
SEQ = 2048
DEPTH = 2
DEC_BATCH = 128
DEC_SEQ = 8
PAST_LEN = 16384
PAGE_SIZE = 128

N_MIXERS = 2
N_RET_LAYERS = (DEPTH + 1) // 2
N_POOL_LAYERS = DEPTH // 2
RET_HEADS = 4
RET_DK = D_MODEL // RET_HEADS
RET_DV = 2 * D_MODEL // RET_HEADS
RET_CHUNK = 128
ROPE_BASE = 10000.0
POOL_WINDOWS = (2, 4, 8, 16)
POOL_GROUPS = len(POOL_WINDOWS)
POOL_GC = D_MODEL // POOL_GROUPS
POOL_BUF = max(POOL_WINDOWS) - 1
N_MEM = 256
X_HEADS = 4
X_HEAD_DIM = D_MODEL // X_HEADS
D_FF = 2816
CONV_W = 3
NORM_EPS = 1e-6
GN_EPS = 1e-5

kernel_name = 'retention_pool_hybrid_decode_step'

F32 = jnp.float32


def rmsnorm(x, g):
    xf = x.astype(F32)
    y = xf * lax.rsqrt(jnp.mean(xf * xf, axis=-1, keepdims=True) + NORM_EPS)
    return (y * g.astype(F32)).astype(x.dtype)


def rope(x, pos):
    d = x.shape[-1]
    inv = 1.0 / (ROPE_BASE ** (jnp.arange(0, d, 2, dtype=F32) / d))
    ang = pos[:, None] * inv[None, :]
    cos = jnp.cos(ang)[None, :, None, :]
    sin = jnp.sin(ang)[None, :, None, :]
    xf = x.astype(F32)
    x1, x2 = xf[..., : d // 2], xf[..., d // 2:]
    return jnp.concatenate([x1 * cos - x2 * sin, x2 * cos + x1 * sin], axis=-1)


def log_decay():
    return jnp.log(1.0 - 2.0 ** (-5.0 - jnp.arange(RET_HEADS, dtype=F32)))


def retention_chunks(q, k, v, s0, chunk):
    B, L, H, dk = q.shape
    dv = v.shape[-1]
    n = L // chunk
    lg = log_decay()
    idx = jnp.arange(chunk, dtype=F32)
    rel = idx[:, None] - idx[None, :]
    inner = jnp.where(rel[None] >= 0, jnp.exp(jnp.maximum(rel, 0.0)[None] * lg[:, None, None]), 0.0)
    q_dec = jnp.exp((idx + 1.0)[None, :] * lg[:, None])[..., None]
    k_dec = jnp.exp((chunk - 1.0 - idx)[None, :] * lg[:, None])[..., None]
    c_dec = jnp.exp(chunk * lg)[:, None, None]

    def blocks(t):
        return t.reshape(B, n, chunk, H, t.shape[-1]).transpose(1, 0, 3, 2, 4)

    def step(s, inp):
        qc, kc, vc = inp
        scores = jnp.einsum('bhqd,bhkd->bhqk', qc, kc) * inner
        o = jnp.einsum('bhqk,bhkv->bhqv', scores, vc) + jnp.einsum('bhqd,bhdv->bhqv', qc * q_dec, s)
        s = s * c_dec + jnp.einsum('bhkd,bhkv->bhdv', kc * k_dec, vc)
        return s, o

    s, o = lax.scan(step, s0, (blocks(q), blocks(k), blocks(v)))
    o = o.transpose(1, 0, 3, 2, 4).reshape(B, L, H, dv)
    return o, s


def retention_mixer(h, pos, s0, w_in, gn_g, w_out):
    B, L, _ = h.shape
    hk = RET_HEADS * RET_DK
    hv = RET_HEADS * RET_DV
    proj = h @ w_in
    q = rope(proj[..., :hk].reshape(B, L, RET_HEADS, RET_DK), pos)
    k = rope(proj[..., hk:2 * hk].reshape(B, L, RET_HEADS, RET_DK), pos) * (RET_DK ** -0.5)
    v = proj[..., 2 * hk:2 * hk + hv].reshape(B, L, RET_HEADS, RET_DV).astype(F32)
    g = proj[..., 2 * hk + hv:].astype(F32)
    chunk = min(RET_CHUNK, L)
    o, s = retention_chunks(q, k, v, s0.astype(F32), chunk)
    mu = jnp.mean(o, axis=-1, keepdims=True)
    var = jnp.mean(jnp.square(o - mu), axis=-1, keepdims=True)
    o = ((o - mu) * lax.rsqrt(var + GN_EPS)).reshape(B, L, hv) * gn_g.astype(F32)
    out = (jax.nn.silu(g) * o).astype(h.dtype) @ w_out
    return out, s


def pool_mixer(h, pos, buf, w_group, scale):
    B, L, D = h.shape
    P = POOL_BUF
    ext = jnp.concatenate([buf.astype(h.dtype), h], axis=1)
    ef = ext.astype(F32)
    cs = jnp.concatenate([jnp.zeros((B, 1, D), F32), jnp.cumsum(ef, axis=1)], axis=1)
    outs = []
    for gi, w in enumerate(POOL_WINDOWS):
        sl = slice(gi * POOL_GC, (gi + 1) * POOL_GC)
        win_sum = cs[:, P + 1:P + L + 1, sl] - cs[:, P + 1 - w:P + L + 1 - w, sl]
        cnt = jnp.minimum(pos + 1.0, float(w))
        outs.append(win_sum / cnt[None, :, None])
    pooled = jnp.concatenate(outs, axis=-1) - h.astype(F32)
    mixed = jnp.einsum('blgc,gcd->blgd', pooled.reshape(B, L, POOL_GROUPS, POOL_GC), w_group.astype(F32))
    out = mixed.reshape(B, L, D) * scale.astype(F32)
    return out.astype(h.dtype), ext[:, -P:]


def mem_kv(mem, g, wk, wv):
    B, M, _ = mem.shape
    hm = rmsnorm(mem, g)
    k = (hm @ wk).reshape(B, M, X_HEADS, X_HEAD_DIM)
    v = (hm @ wv).reshape(B, M, X_HEADS, X_HEAD_DIM)
    return k, v


def cross_attn(h, k, v, wq, wo):
    B, L, _ = h.shape
    q = (h @ wq).reshape(B, L, X_HEADS, X_HEAD_DIM)
    s = jnp.einsum('blhd,bmhd->bhlm', q.astype(F32), k.astype(F32)) * (X_HEAD_DIM ** -0.5)
    p = jax.nn.softmax(s, axis=-1)
    o = jnp.einsum('bhlm,bmhd->blhd', p, v.astype(F32)).reshape(B, L, D_MODEL)
    return o.astype(h.dtype) @ wo


def conv_ffn(h, buf, w_up, cw, cb, w_down):
    L = h.shape[1]
    u = h @ w_up
    ext = jnp.concatenate([buf.astype(u.dtype), u], axis=1)
    c = cb
    for j in range(CONV_W):
        c = c + cw[j] * ext[:, j:j + L]
    a, gate = c[..., :D_FF], c[..., D_FF:]
    y = (a * jax.nn.silu(gate)) @ w_down
    return y, ext[:, -(CONV_W - 1):]


def trunk(x, pos, ret_s0, pool_b0, conv_b0, mem_k, mem_v, w_ret_in, ret_gn, w_ret_out, pool_w, pool_scale,
          w_xq, w_xo, w_up, conv_w, conv_b, w_down, norm_mix, norm_xattn, norm_ffn, norm_final):
    new_ret, new_pool, new_conv = [], [], []
    for i in range(DEPTH):
        j = i // N_MIXERS
        h = rmsnorm(x, norm_mix[i])
        if i % N_MIXERS == 0:
            out, s = retention_mixer(h, pos, ret_s0[j], w_ret_in[j], ret_gn[j], w_ret_out[j])
            new_ret.append(s)
        else:
            out, b = pool_mixer(h, pos, pool_b0[j], pool_w[j], pool_scale[j])
            new_pool.append(b)
        x = x + out
        x = x + cross_attn(rmsnorm(x, norm_xattn[i]), mem_k[i], mem_v[i], w_xq[i], w_xo[i])
        y, cbuf = conv_ffn(rmsnorm(x, norm_ffn[i]), conv_b0[i], w_up[i], conv_w[i], conv_b[i], w_down[i])
        new_conv.append(cbuf)
        x = x + y
    return rmsnorm(x, norm_final), jnp.stack(new_ret), jnp.stack(new_pool), jnp.stack(new_conv)


def setup_inputs(seed: int = 0) -> dict:
    key = jax.random.key(seed)
    ks = jax.random.split(key, 32)
    nrm = lambda k, shape, s: jax.random.normal(k, shape, F32) * s
    hk = RET_HEADS * RET_DK
    hv = RET_HEADS * RET_DV
    return {
        'x_prompt': nrm(ks[0], (BATCH, SEQ, D_MODEL), 1.0),
        'x_sample': nrm(ks[1], (DEC_BATCH, DEC_SEQ, D_MODEL), 1.0),
        'mem_prompt': nrm(ks[2], (BATCH, N_MEM, D_MODEL), 1.0),
        'cache_mem_k': nrm(ks[3], (DEPTH, DEC_BATCH, N_MEM, X_HEADS, X_HEAD_DIM), 1.0),
        'cache_mem_v': nrm(ks[4], (DEPTH, DEC_BATCH, N_MEM, X_HEADS, X_HEAD_DIM), 1.0),
        'state_ret': nrm(ks[5], (N_RET_LAYERS, DEC_BATCH, RET_HEADS, RET_DK, RET_DV), 0.1),
        'cache_pool': nrm(ks[6], (N_POOL_LAYERS, DEC_BATCH, POOL_BUF, D_MODEL), 1.0),
        'cache_ffn_conv': nrm(ks[7], (DEPTH, DEC_BATCH, CONV_W - 1, 2 * D_FF), 1.0),
        'w_ret_in': nrm(ks[8], (N_RET_LAYERS, D_MODEL, 2 * hk + 2 * hv), D_MODEL ** -0.5),
        'ret_gn': 1.0 + nrm(ks[9], (N_RET_LAYERS, hv), 0.02),
        'w_ret_out': nrm(ks[10], (N_RET_LAYERS, hv, D_MODEL), hv ** -0.5),
        'pool_w': nrm(ks[11], (N_POOL_LAYERS, POOL_GROUPS, POOL_GC, POOL_GC), POOL_GC ** -0.5),
        'pool_scale': 1.0 + nrm(ks[12], (N_POOL_LAYERS, D_MODEL), 0.02),
        'norm_mem': 1.0 + nrm(ks[13], (DEPTH, D_MODEL), 0.02),
        'w_xq': nrm(ks[14], (DEPTH, D_MODEL, D_MODEL), D_MODEL ** -0.5),
        'w_xk': nrm(ks[15], (DEPTH, D_MODEL, D_MODEL), D_MODEL ** -0.5),
        'w_xv': nrm(ks[16], (DEPTH, D_MODEL, D_MODEL), D_MODEL ** -0.5),
        'w_xo': nrm(ks[17], (DEPTH, D_MODEL, D_MODEL), D_MODEL ** -0.5),
        'w_up': nrm(ks[18], (DEPTH, D_MODEL, 2 * D_FF), D_MODEL ** -0.5),
        'conv_w': nrm(ks[19], (DEPTH, CONV_W, 2 * D_FF), CONV_W ** -0.5),
        'conv_b': nrm(ks[20], (DEPTH, 2 * D_FF), 0.02),
        'w_down': nrm(ks[21], (DEPTH, D_FF, D_MODEL), D_FF ** -0.5),
        'norm_mix': 1.0 + nrm(ks[22], (DEPTH, D_MODEL), 0.02),
        'norm_xattn': 1.0 + nrm(ks[23], (DEPTH, D_MODEL), 0.02),
        'norm_ffn': 1.0 + nrm(ks[24], (DEPTH, D_MODEL), 0.02),
        'norm_final': 1.0 + nrm(ks[25], (D_MODEL,), 0.02),
    }


def reference(x_prompt, x_sample, mem_prompt, cache_mem_k, cache_mem_v, state_ret, cache_pool, cache_ffn_conv,
              w_ret_in, ret_gn, w_ret_out, pool_w, pool_scale, norm_mem, w_xq, w_xk, w_xv, w_xo,
              w_up, conv_w, conv_b, w_down, norm_mix, norm_xattn, norm_ffn, norm_final):
    B = x_prompt.shape[0]
    pos_p = jnp.arange(x_prompt.shape[1], dtype=F32)
    pos_s = PAST_LEN + jnp.arange(x_sample.shape[1], dtype=F32)

    kvs = [mem_kv(mem_prompt, norm_mem[i], w_xk[i], w_xv[i]) for i in range(DEPTH)]
    mem_k_p = jnp.stack([kv[0] for kv in kvs])
    mem_v_p = jnp.stack([kv[1] for kv in kvs])
    ret0 = jnp.zeros((N_RET_LAYERS, B, RET_HEADS, RET_DK, RET_DV), F32)
    pool0 = jnp.zeros((N_POOL_LAYERS, B, POOL_BUF, D_MODEL), x_prompt.dtype)
    conv0 = jnp.zeros((DEPTH, B, CONV_W - 1, 2 * D_FF), x_prompt.dtype)
    y_prompt, ret_p, pool_p, conv_p = trunk(
        x_prompt, pos_p, ret0, pool0, conv0, mem_k_p, mem_v_p, w_ret_in, ret_gn, w_ret_out, pool_w, pool_scale,
        w_xq, w_xo, w_up, conv_w, conv_b, w_down, norm_mix, norm_xattn, norm_ffn, norm_final)

    y_sample, ret_s, pool_s, conv_s = trunk(
        x_sample, pos_s, state_ret, cache_pool, cache_ffn_conv, cache_mem_k, cache_mem_v, w_ret_in, ret_gn,
        w_ret_out, pool_w, pool_scale, w_xq, w_xo, w_up, conv_w, conv_b, w_down, norm_mix, norm_xattn,
        norm_ffn, norm_final)

    return (y_prompt, y_sample, ret_p, ret_s.astype(state_ret.dtype), pool_p, pool_s, conv_p, conv_s, mem_k_p, mem_v_p)
```

```python
import math
import numpy as np
import concourse.bass as bass
import concourse.mybir as mybir
from concourse.bass_utils import run_bass_kernel_spmd

F32 = mybir.dt.float32
BF16 = mybir.dt.bfloat16
AF = mybir.ActivationFunctionType
ALU = mybir.AluOpType

NCORES = 8
D = 1024
KC = 8
SEQ = 2048
NS = 16
LS = 8
NT = SEQ + NS * LS
TILES = [(0, 512), (512, 512), (1024, 512), (1536, 512), (2048, 128)]
H = 4
DK = 256
DV = 512
NMEM = 256
DFF = 2816
NFC = DFF // 128
PAST = 16384
EPS = 1e-6
GN_EPS = 1e-5
WSLOT = 4096
NSLOT = 6
SLOTB = 512

V_MIX, V_XA, V_FFN, V_FIN, V_MEM, V_PSC = 0, 16, 32, 48, 56, 72
V_CW = 80
V_CB = V_CW + 6 * 44
NVEC = V_CB + 2 * 44


class Sch:
    LIMIT = 12000
    R = 6

    def __init__(self, nc):
        self.nc = nc
        self.E = {'pe': nc.tensor, 'act': nc.scalar, 'dve': nc.vector, 'pool': nc.gpsimd, 'sp': nc.sync}
        self.sem = {}
        self.cnt = {}
        self.nsem = 0
        for e in ('pe', 'act', 'dve', 'pool'):
            self.sem[e] = self._newsem(e)
            self.cnt[e] = 0
        self.dq = {}
        for q in ('sp', 'pool'):
            self.dq[q] = {'sems': [self._newsem('d' + q) for _ in range(self.R)], 'vals': [0] * self.R, 'i': 0}
        self.seen = {e: {} for e in self.E}
        self.lastw = {}
        self.reads = {}
        self.nwait = 0
        self.nins = 0
        self.ps_state = {}
        self.know = {}

    def _newsem(self, tag):
        self.nsem += 1
        return self.nc.alloc_semaphore('%s_%d' % (tag, self.nsem))

    @staticmethod
    def _exp(keys):
        out = []
        for k in keys:
            if isinstance(k, list):
                out.extend(k)
            else:
                out.append(k)
        return out

    def _need(self, src, reads, writes):
        need = {}

        def add(ev):
            sem, val, _ = ev
            if need.get(sem, 0) < val:
                need[sem] = val
        for r in reads:
            for ev in self.lastw.get(r, ()):
                add(ev)
        same_ok = (src == 'pe')
        for w in writes:
            for ev in self.lastw.get(w, ()):
                if ev[2] != src or not same_ok:
                    add(ev)
            rd = self.reads.get(w)
            if rd:
                for ev in rd.values():
                    if ev[2] != src or not same_ok:
                        add(ev)
        return need

    def _waits(self, eng, need, attach=False):
        seen = self.seen[eng]
        todo = []
        for sem, val in sorted(need.items(), key=lambda kv: -len(self.know.get((kv[0], kv[1]), ()))):
            if seen.get(sem, 0) >= val:
                continue
            todo.append((sem, val))
            seen[sem] = val
            for s2, v2 in self.know.get((sem, val), {}).items():
                if seen.get(s2, 0) < v2:
                    seen[s2] = v2
        last = None
        if attach and todo:
            last = todo.pop()
        for sem, val in todo:
            self.E[eng].wait_ge(sem, val)
            self.nwait += 1
        return last

    def _record(self, ev, reads, writes):
        sem, val, src = ev
        q = src[4:] if src.startswith('dma_') else src
        self.know[(sem, val)] = dict(self.seen[q])
        for w in writes:
            if isinstance(w, tuple) and w[0] == 'ps':
                self.ps_state[w] = 'w' if src == 'pe' else 'r'
        for r in reads:
            self.reads.setdefault(r, {})[(src, sem)] = ev
        for w in writes:
            if src.startswith('dma') and not self.reads.get(w):
                self.lastw[w] = [p for p in self.lastw.get(w, ()) if p[2] == src] + [ev]
            else:
                self.lastw[w] = [ev]
            self.reads[w] = {}

    def _bump(self, eng, ins):
        self.cnt[eng] += 1
        ins.then_inc(self.sem[eng], 1)
        ev = (self.sem[eng], self.cnt[eng], eng)
        if self.cnt[eng] >= self.LIMIT:
            self.sem[eng] = self._newsem(eng)
            self.cnt[eng] = 0
        return ev

    def _pexcl(self, reads, writes):
        ex = [r for r in reads if isinstance(r, tuple) and r[0] == 'ps' and r not in writes]
        return writes + ex

    def op(self, eng, fn, reads=(), writes=()):
        reads = self._exp(reads)
        writes = self._pexcl(reads, self._exp(writes))
        last = self._waits(eng, self._need(eng, reads, writes), attach=True)
        ins = fn(self.E[eng])
        if last is not None:
            ins._wait_ge(last[0], last[1])
        self.nins += 1
        self._record(self._bump(eng, ins), reads, writes)

    def group(self, eng, fns, reads=(), writes=()):
        reads = self._exp(reads)
        writes = self._pexcl(reads, self._exp(writes))
        last = self._waits(eng, self._need(eng, reads, writes), attach=True)
        ins = None
        for fn in fns:
            ins = fn(self.E[eng])
            if last is not None:
                ins._wait_ge(last[0], last[1])
                last = None
            self.nins += 1
        self._record(self._bump(eng, ins), reads, writes)

    def dma(self, q, out, in_, reads=(), writes=()):
        reads = self._exp(reads)
        writes = self._exp(writes)
        src = 'dma_' + q
        need = self._need(src, reads, writes)
        d = self.dq[q]
        j = d['i'] % self.R
        d['i'] += 1
        sem = d['sems'][j]
        if d['vals'][j] > 0 and need.get(sem, 0) < d['vals'][j]:
            need[sem] = d['vals'][j]
        self._waits(q, need)
        ins = self.E[q].dma_start(out=out, in_=in_)
        ins.then_inc(sem, 16)
        self.nins += 1
        d['vals'][j] += 16
        self._record((sem, d['vals'][j], src), reads, writes)

    def finish(self):
        need = {}
        for q in self.dq.values():
            for s, v in zip(q['sems'], q['vals']):
                if v > 0:
                    need[s] = v
        for e in ('pe', 'act', 'dve', 'pool'):
            if self.cnt[e] > 0:
                need[self.sem[e]] = self.cnt[e]
        self._waits('sp', need)


class Arena:
    def __init__(self, nc, nbytes):
        self.nc = nc
        self.h = nc.alloc_sbuf_tensor('arena', [128, nbytes // 4], F32)
        self.base = nc.lookup_mloc(self.h).addr
        self.size = nbytes
        self.off = 0
        self.n = 0

    def reset(self, off=0):
        self.off = off

    def mark(self):
        return self.off

    def alloc(self, name, shape, dtype):
        esz = 4 if dtype == F32 else 2
        nb = esz
        for s in shape[1:]:
            nb *= s
        nb = (nb + 31) // 32 * 32
        self.off = (self.off + SLOTB - 1) // SLOTB * SLOTB
        assert self.off + nb <= self.size, (name, self.off, nb, self.size)
        self.n += 1
        t = self.nc.alloc_sbuf_tensor_at('%s_%d' % (name, self.n), list(shape), dtype, offset=self.base + self.off)
        keys = [('ar', s) for s in range(self.off // SLOTB, (self.off + nb - 1) // SLOTB + 1)]
        self.off += nb
        return t, keys


def build_program():
    nc = bass.Bass("TRN2", target_bir_lowering=False)

    def din(name, shape):
        return nc.dram_tensor(name, list(shape), F32, kind="ExternalInput").ap()

    def dout(name, shape):
        return nc.dram_tensor(name, list(shape), F32, kind="ExternalOutput").ap()

    x_d = din("x", [SEQ, D])
    xs_d = din("xs", [NS * LS, D])
    mem_d = din("mem", [NMEM, D])
    cmk_d = din("cmk", [2, NS, NMEM, D])
    cmv_d = din("cmv", [2, NS, NMEM, D])
    sret_d = din("sret", [NS, H, DK, DV])
    cpool_d = din("cpool", [NS * 15, D])
    cconv_d = din("cconv", [2, NS * 2, 2 * DFF])
    w_in_d = din("w_in", [D, 6144])
    gn_d = din("gn", [1, 2048])
    w_out_d = din("w_out", [2048, D])
    pw_d = din("pw", [4, 256, 256])
    wq_d = din("wq", [2, D, D])
    wk_d = din("wk", [2, D, D])
    wv_d = din("wv", [2, D, D])
    wo_d = din("wo", [2, D, D])
    wup_d = din("wup", [2, D, 2 * DFF])
    wdn_d = din("wdn", [2, DFF, D])
    vec_d = din("vec", [128, NVEC])
    cst_d = din("cst", [128, 128 + 16 + 2 + 248 + 256])
    cos_d = din("cosT", [128, NT])
    sin_d = din("sinT", [128, NT])
    hcf_d = din("hcf", [H, 128, 258])
    hcq_d = din("hcq", [H, 128, 640])
    icn_d = din("icn", [128, 4 * 512])
    pmat_d = din("pmat", [128, 24 * 128])

    y_d = dout("y", [SEQ, D])
    ys_d = dout("ys", [NS * LS, D])
    rsp_d = dout("rsp", [H, DK, DV])
    rss_d = dout("rss", [NS, H, DK, DV])
    pbp_d = dout("pbp", [15, D])
    pbs_d = dout("pbs", [NS, 15, D])
    cbp_d = dout("cbp", [2, 2, 2 * DFF])
    cbs_d = dout("cbs", [2, NS * 2, 2 * DFF])
    mkp_d = dout("mkp", [2, NMEM, D])
    mvp_d = dout("mvp", [2, NMEM, D])

    S = Sch(nc)

    xT = nc.alloc_sbuf_tensor("sb_xT", [128, KC, NT], F32)
    hT = nc.alloc_sbuf_tensor("sb_hT", [128, KC, NT], BF16)
    wsl = [nc.alloc_sbuf_tensor("wsl%d" % i, [128, WSLOT], BF16) for i in range(NSLOT)]
    vec = nc.alloc_sbuf_tensor("sb_vec", [128, NVEC], F32)
    cst = nc.alloc_sbuf_tensor("sb_cst", [128, 128 + 16 + 2], F32)
    cstb = nc.alloc_sbuf_tensor("sb_cstb", [128, 128 + 248 + 256], BF16)
    ident = cst[:, 0:128]
    rowmask = cst[:, 128:144]
    epsc = cst[:, 144:145]
    gepsc = cst[:, 145:146]
    identb = cstb[:, 0:128]
    colmask = cstb[:, 128:376]
    onesD = cstb[:, 376:504]
    ones1 = cstb[:, 504:632]
    remaining = nc.sbuf_bytes_remaining
    AR = Arena(nc, (remaining - 64) // SLOTB * SLOTB)

    psb = [nc.alloc_psum_tensor("ps%d" % i, [128, 512], F32) for i in range(7)]
    psbf = nc.alloc_psum_tensor("psbf", [128, 1024], BF16)
    PSBK = ('ps', 7)
    ps_i = [0]
    NROT = [6]

    def ps_set(n):
        NROT[0] = n

    def ps_next():
        i = ps_i[0] % NROT[0]
        ps_i[0] += 1
        assert S.ps_state.get(('ps', i)) != 'w', ("PSUM bank re-allocated before its consumer was emitted", i)
        return psb[i], ('ps', i)
    PSL = [(psb[4], ('ps', 4)), (psb[5], ('ps', 5))]
    PST, PSTK = psb[6], ('ps', 6)

    def xk(t):
        return ('xT', t)

    def hk(t):
        return ('hT', t)

    def mm(out, lhsT, rhs, start, stop):
        return lambda e: e.matmul(out, lhsT, rhs, start=start, stop=stop)

    def tr(out, in_, idn):
        return lambda e: e.transpose(out, in_, idn)

    wnext = [0]

    def wload(src_ap_list):
        i = wnext[0] % NSLOT
        wnext[0] += 1
        key = ('W', i)
        for (c0, nk, ncol, ap) in src_ap_list:
            dst = wsl[i][:, c0:c0 + nk * ncol].rearrange("p (k n) -> p k n", k=nk)
            S.dma('pool', dst, ap, writes=[key])
        return wsl[i], key

    def wview(slot, nk, ncol, c0=0):
        return slot[:, c0:c0 + nk * ncol].rearrange("p (k n) -> p k n", k=nk)

    def wload2(nk, ncol, parts):
        i = wnext[0] % NSLOT
        wnext[0] += 1
        key = ('W', i)
        v = wview(wsl[i], nk, ncol)
        for (sel, ap) in parts:
            S.dma('pool', sel(v), ap, writes=[key])
        return v, key

    def kpn(ap):
        return ap.rearrange("(k p) n -> p k n", p=128)

    ALL = lambda v: v

    S.dma('sp', vec[:], vec_d, writes=['vec'])
    S.dma('sp', cst[:], cst_d[:, 0:146], writes=['cst'])
    S.dma('pool', cstb[:, 0:128], cst_d[:, 0:128], writes=['cstb'])
    S.dma('pool', cstb[:, 128:632], cst_d[:, 146:650], writes=['cstb'])

    def norm_stats(src, srckeys, c0, n, tmp):
        i = tmp['i']
        tmp['i'] += 1
        sq, sqk = tmp['sq'][i % len(tmp['sq'])]
        ms, msk = tmp['ms'][i % len(tmp['ms'])]
        rs, rsk = tmp['rs'][i % len(tmp['rs'])]
        pst, pk = ps_next()
        S.op('act', lambda e: e.activation(sq[:, :, 0:n], src[:, :, c0:c0 + n], AF.Square), reads=srckeys, writes=sqk)
        S.group('pe', [mm(pst[:, 0:n], onesD, sq[:, kc, 0:n], kc == 0, kc == KC - 1) for kc in range(KC)],
                reads=sqk + ['cstb'], writes=[pk])
        S.op('act', lambda e: e.activation(ms[:, 0:n], pst[:, 0:n], AF.Ln, bias=epsc[:, 0:1]), reads=[pk, 'cst'], writes=msk)
        S.op('act', lambda e: e.activation(rs[:, 0:n], ms[:, 0:n], AF.Exp, scale=-0.5), reads=msk, writes=rsk)
        return rs, rsk

    def norm_cols(src, srckeys, c0, n, gcol, dst_fn, dstkeys, tmp):
        rs, rsk = norm_stats(src, srckeys, c0, n, tmp)
        for kc in range(KC):
            S.op('dve', lambda e, kc=kc: e.scalar_tensor_tensor(
                dst_fn(kc), src[:, kc, c0:c0 + n], vec[:, gcol + kc:gcol + kc + 1], rs[:, 0:n], ALU.mult, ALU.mult),
                reads=srckeys + [rsk, 'vec'], writes=dstkeys)

    def norm_tmp(w=512, nb=2):
        return {'sq': [AR.alloc('sq', [128, KC, w], BF16) for _ in range(nb)],
                'ms': [AR.alloc('ms', [128, w], F32) for _ in range(nb)],
                'rs': [AR.alloc('rstd', [128, w], F32) for _ in range(nb)], 'i': 0, 'w': w}

    def norm_all(gcol):
        AR.reset()
        tmp = norm_tmp()
        for t, (t0, n) in enumerate(TILES):
            norm_cols(xT, [xk(t)], t0, n, gcol, lambda kc, t0=t0, n=n: hT[:, kc, t0:t0 + n], [hk(t)], tmp)

    def hoist_norm(gcol, t, tmp):
        t0, n = TILES[t]
        w = tmp['w']
        for c0 in range(t0, t0 + n, w):
            m = min(w, t0 + n - c0)
            norm_cols(xT, [xk(t)], c0, m, gcol, lambda kc, c0=c0, m=m: hT[:, kc, c0:c0 + m], [hk(t)], tmp)

    def io_in(nn=None):
        AR.reset()
        xin = [AR.alloc('xin', [128, D], F32) for _ in range(3)]
        ntmp = norm_tmp(512, 1) if nn is not None else None
        for i in range(17):
            xi, xik = xin[i % 3]
            src = x_d[i * 128:(i + 1) * 128, :] if i < 16 else xs_d
            S.dma('sp', xi[:], src, writes=[xik])
            for half in range(2):
                ptt, ptk = ps_next()
                S.group('pe', [tr(ptt[:, j * 128:(j + 1) * 128], xi[:, (half * 4 + j) * 128:(half * 4 + j + 1) * 128], ident)
                               for j in range(4)], reads=[xik, 'cst'], writes=[ptk])
                S.op('act', lambda e, half=half, i=i, ptt=ptt: e.activation(
                    xT[:, half * 4:half * 4 + 4, i * 128:(i + 1) * 128],
                    ptt[:].rearrange("p (j n) -> p j n", j=4), AF.Copy),
                    reads=[ptk], writes=[xk(i // 4)])
            if nn is not None and (i % 4 == 3 or i == 16):
                hoist_norm(nn, i // 4, ntmp)

    def memkv(li):
        AR.reset()
        KT = AR.alloc('KT', [128, KC, NMEM], BF16)
        Vb = AR.alloc('Vb', [128, 2, D], BF16)
        keep = AR.mark()
        tmp = norm_tmp(NMEM, 1)
        memin = AR.alloc('memin', [128, 2, D], F32)
        memT = AR.alloc('memT', [128, KC, NMEM], F32)
        hmT = AR.alloc('hmT', [128, KC, NMEM], BF16)
        kst = [AR.alloc('kst', [128, D], F32) for _ in range(2)]
        wks = [wload([(0, KC, 512, wk_d[li][:, hf * 512:(hf + 1) * 512].rearrange("(k p) n -> p k n", p=128))]) for hf in range(2)]
        wvs = [wload([(0, KC, 512, wv_d[li][:, hf * 512:(hf + 1) * 512].rearrange("(k p) n -> p k n", p=128))]) for hf in range(2)]
        S.dma('sp', memin[0][:], mem_d.rearrange("(m p) f -> p m f", p=128), writes=memin[1])
        for mc in range(2):
            for half in range(2):
                S.group('pe', [tr(PST[:, j * 128:(j + 1) * 128], memin[0][:, mc, (half * 4 + j) * 128:(half * 4 + j + 1) * 128], ident)
                               for j in range(4)], reads=memin[1] + ['cst'], writes=[PSTK])
                S.op('act', lambda e, half=half, mc=mc: e.activation(
                    memT[0][:, half * 4:half * 4 + 4, mc * 128:(mc + 1) * 128],
                    PST[:].rearrange("p (j n) -> p j n", j=4), AF.Copy), reads=[PSTK], writes=memT[1])
        norm_cols(memT[0], memT[1], 0, NMEM, V_MEM + li * 8, lambda kc: hmT[0][:, kc, :], hmT[1], tmp)
        ki = 0
        for (wsx, out_d, isv) in ((wks, mkp_d, False), (wvs, mvp_d, True)):
            for mc in range(2):
                ks, ksk = kst[ki % 2]
                ki += 1
                for hf in range(2):
                    w, wkey = wsx[hf]
                    wv_ = wview(w, KC, 512)
                    pt, pk = ps_next()
                    S.group('pe', [mm(pt[:, :], hmT[0][:, kc, mc * 128:(mc + 1) * 128], wv_[:, kc, :], kc == 0, kc == KC - 1)
                                   for kc in range(KC)], reads=hmT[1] + [wkey], writes=[pk])
                    S.op('act', lambda e, ks=ks, hf=hf, pt=pt: e.activation(ks[:, hf * 512:(hf + 1) * 512], pt[:, :], AF.Copy),
                         reads=[pk], writes=ksk)
                    if isv:
                        S.op('dve', lambda e, hf=hf, ks=ks, mc=mc: e.tensor_copy(Vb[0][:, mc, hf * 512:(hf + 1) * 512], ks[:, hf * 512:(hf + 1) * 512]),
                             reads=ksk, writes=Vb[1])
                S.dma('sp', out_d[li][mc * 128:(mc + 1) * 128, :], ks[:], reads=ksk)
        for cc in range(KC):
            w, wkey = wks[cc // 4]
            wv_ = wview(w, KC, 512)
            pt, pk = ps_next()
            S.group('pe', [mm(pt[:, 0:NMEM], wv_[:, kc, (cc % 4) * 128:(cc % 4 + 1) * 128], hmT[0][:, kc, :], kc == 0, kc == KC - 1)
                           for kc in range(KC)], reads=hmT[1] + [wkey], writes=[pk])
            S.op('act', lambda e, cc=cc, pt=pt: e.activation(KT[0][:, cc, :], pt[:, 0:NMEM], AF.Copy), reads=[pk], writes=KT[1])
        return KT, Vb, keep

    def resid_add(pt, pk, dc, t, t0, n, scol=None):
        if scol is None:
            S.op('dve', lambda e: e.tensor_tensor(xT[:, dc, t0:t0 + n], pt[:, 0:n], xT[:, dc, t0:t0 + n], ALU.add),
                 reads=[pk, xk(t)], writes=[xk(t)])
        else:
            S.op('dve', lambda e: e.scalar_tensor_tensor(xT[:, dc, t0:t0 + n], pt[:, 0:n], scol, xT[:, dc, t0:t0 + n],
                                                        ALU.mult, ALU.add),
                 reads=[pk, xk(t), 'vec'], writes=[xk(t)])

    def retention_head(h, nn=None):
        lg = math.log(1.0 - 2.0 ** (-5.0 - h))
        cdecP = math.exp(128 * lg)
        cdecS = math.exp(LS * lg)
        AR.reset()
        wqk, wqkk = wload2(KC, 512, [(lambda v: v[:, :, 0:256], kpn(w_in_d[:, h * 256:(h + 1) * 256])),
                                     (lambda v: v[:, :, 256:512], kpn(w_in_d[:, 1024 + h * 256:1024 + (h + 1) * 256]))])
        wv, wvk = wload2(KC, 512, [(ALL, kpn(w_in_d[:, 2048 + h * 512:2048 + (h + 1) * 512]))])
        wg, wgk = wload2(KC, 512, [(ALL, kpn(w_in_d[:, 4096 + h * 512:4096 + (h + 1) * 512]))])
        wo, wok = wload2(4, 1024, [(ALL, kpn(w_out_d[h * 512:(h + 1) * 512, :]))])
        hcf, hcfk = AR.alloc('hcf', [128, 258], F32)
        hcq, hcqk = AR.alloc('hcq', [128, 640], BF16)
        gng, gngk = AR.alloc('gng', [128, 512], F32)
        S.dma('sp', hcf[:], hcf_d[h], writes=hcfk)
        S.dma('pool', hcq[:], hcq_d[h], writes=hcqk)
        S.dma('sp', gng[:], gn_d[0:1, h * 512:(h + 1) * 512].partition_broadcast(128), writes=gngk)
        AR.alloc('al', [128, 8], F32)
        o_cs = AR.mark()
        cos, cosk = AR.alloc('cos', [128, 512], F32)
        sin, sink = AR.alloc('sin', [128, 512], F32)
        ta, tak = AR.alloc('ta', [128, 512], F32)
        tb, tbk = AR.alloc('tb', [128, 512], F32)
        qT, qTk = AR.alloc('qT', [128, 2, 512], BF16)
        kT, kTk = AR.alloc('kT', [128, 2, 512], BF16)
        qdT, qdTk = AR.alloc('qdT', [128, 2, 512], BF16)
        vS, vSk = AR.alloc('vS', [128, 4, 512], BF16)
        sg, sgk = AR.alloc('sg', [128, 4, 512], BF16)
        ATs = [AR.alloc('AT', [128, 128], BF16) for _ in range(2)]
        kds = [AR.alloc('kd', [128, 256], BF16) for _ in range(2)]
        sts = [AR.alloc('st', [128, 16], F32) for _ in range(2)]
        ons = [AR.alloc('on', [128, 512], BF16) for _ in range(2)]
        ogs = [AR.alloc('og', [128, 512], BF16) for _ in range(2)]
        ogT, ogTk = AR.alloc('ogT', [128, 4, 512], BF16)
        mk_ = AR.mark()
        Sf, Sfk = AR.alloc('Sf', [128, 2, 512], F32)
        Sb, Sbk = AR.alloc('Sb', [128, 2, 512], BF16)
        AR.reset(mk_)
        S0bs = [AR.alloc('S0b', [128, 2, 512], BF16) for _ in range(2)]
        S0fs = [AR.alloc('S0f', [128, 2, 512], F32) for _ in range(2)]
        sv_ = AR.mark()
        AR.reset(o_cs)
        S0fs += [AR.alloc('S0f', [128, 2, 512], F32) for _ in range(2)]
        ntmp = None
        if nn is not None:
            AR.reset(o_cs)
            ntmp = norm_tmp(256, 1)
        AR.reset(sv_)
        Zbs = [AR.alloc('Zb', [128, 2, 128], BF16) for _ in range(2)]
        kdzs = [AR.alloc('kdz', [128, 256], BF16) for _ in range(2)]
        cnt = [0]

        CH = {}

        def projA(t):
            t0, n = TILES[t]
            sample = (t == 4)
            nsub = n // 128
            S.dma('sp', cos[:, 0:n], cos_d[:, t0:t0 + n], writes=cosk)
            S.dma('sp', sin[:, 0:n], sin_d[:, t0:t0 + n], writes=sink)
            pq = []
            for cc in range(4):
                pt, pk = ps_next()
                S.group('pe', [mm(pt[:, 0:n], wqk[:, kc, cc * 128:(cc + 1) * 128], hT[:, kc, t0:t0 + n], kc == 0, kc == KC - 1)
                               for kc in range(KC)], reads=[hk(t), wqkk], writes=[pk])
                pq.append((pt, pk))
            for which, (dst, dkey) in enumerate(((qT, qTk), (kT, kTk))):
                (pa, pak), (pb, pbk) = pq[2 * which], pq[2 * which + 1]
                S.op('dve', lambda e: e.tensor_tensor(ta[:, 0:n], pa[:, 0:n], cos[:, 0:n], ALU.mult), reads=[pak, cosk], writes=tak)
                S.op('dve', lambda e: e.tensor_tensor(tb[:, 0:n], pb[:, 0:n], sin[:, 0:n], ALU.mult), reads=[pbk, sink], writes=tbk)
                S.op('dve', lambda e: e.tensor_tensor(dst[:, 0, 0:n], ta[:, 0:n], tb[:, 0:n], ALU.subtract), reads=[tak, tbk], writes=dkey)
                S.op('dve', lambda e: e.tensor_tensor(ta[:, 0:n], pb[:, 0:n], cos[:, 0:n], ALU.mult), reads=[pbk, cosk], writes=tak)
                S.op('dve', lambda e: e.tensor_tensor(tb[:, 0:n], pa[:, 0:n], sin[:, 0:n], ALU.mult), reads=[pak, sink], writes=tbk)
                S.op('dve', lambda e: e.tensor_tensor(dst[:, 1, 0:n], ta[:, 0:n], tb[:, 0:n], ALU.add), reads=[tak, tbk], writes=dkey)
            qo = 512 if sample else 0
            for c in range(2):
                S.op('dve', lambda e, c=c: e.tensor_tensor(qdT[:, c, 0:n], qT[:, c, 0:n], hcq[:, qo:qo + n], ALU.mult),
                     reads=[qTk, hcqk], writes=qdTk)
            for s_ in range(nsub):
                c0 = t0 + s_ * 128
                pt, pk = ps_next()
                S.group('pe', [mm(pt[:, :], hT[:, kc, c0:c0 + 128], wv[:, kc, :], kc == 0, kc == KC - 1) for kc in range(KC)],
                        reads=[hk(t), wvk], writes=[pk])
                S.op('act', lambda e, s_=s_, pt=pt: e.activation(vS[:, s_, :], pt[:, :], AF.Copy), reads=[pk], writes=vSk)

        def projA2(t):
            t0, n = TILES[t]
            nsub = n // 128
            for s_ in range(nsub):
                c0 = t0 + s_ * 128
                pt, pk = ps_next()
                S.group('pe', [mm(pt[:, :], hT[:, kc, c0:c0 + 128], wg[:, kc, :], kc == 0, kc == KC - 1) for kc in range(KC)],
                        reads=[hk(t), wgk], writes=[pk])
                S.op('act', lambda e, s_=s_, pt=pt: e.activation(sg[:, s_, :], pt[:, :], AF.Silu), reads=[pk], writes=sgk)
                S.op('dve', lambda e, s_=s_: e.tensor_tensor(sg[:, s_, :], sg[:, s_, :], gng[:, :], ALU.mult), reads=sgk + gngk, writes=sgk)

        def st1(t, s_):
            sample = (t == 4)
            cs = slice(s_ * 128, (s_ + 1) * 128)
            mask = hcf[:, 128:256] if sample else hcf[:, 0:128]
            kdc = hcf[:, 257:258] if sample else hcf[:, 256:257]
            i2 = cnt[0] % 2
            cnt[0] += 1
            AT, ATk = ATs[i2]
            kd, kdk = kds[i2]
            pt, pk = ps_next()
            S.group('pe', [mm(pt[:, 0:128], kT[:, c, cs], qT[:, c, cs], c == 0, c == 1) for c in range(2)],
                    reads=[kTk, qTk], writes=[pk])
            S.op('dve', lambda e: e.tensor_tensor(AT[:, :], pt[:, 0:128], mask, ALU.mult), reads=[pk, hcfk], writes=ATk)
            S.group('pe', [tr(psbf[:, c * 128:(c + 1) * 128], kT[:, c, cs], identb) for c in range(2)],
                    reads=[kTk, 'cstb'], writes=[PSBK])
            S.op('act', lambda e: e.activation(kd[:, :], psbf[:, 0:256], AF.Identity, scale=kdc), reads=[PSBK, hcfk], writes=kdk)
            CH[(t, s_)] = dict(i2=i2, AT=AT, ATk=ATk, kd=kd, kdk=kdk)

        def st2(t, s_):
            sample = (t == 4)
            chunk = t * 4 + s_
            cs = slice(s_ * 128, (s_ + 1) * 128)
            c_ = CH[(t, s_)]
            AT, ATk, kd, kdk = c_['AT'], c_['ATk'], c_['kd'], c_['kdk']
            if not sample:
                po, pok = PSL[s_ % 2]
                for c in range(2):
                    pn, pnk = ps_next()
                    S.op('pe', mm(pn[:, :], kd[:, c * 128:(c + 1) * 128], vS[:, s_, :], True, True), reads=[kdk, vSk], writes=[pnk])
                    c_.setdefault('pn', []).append((pn, pnk))
            else:
                po, pok = PSL[0]
                S.op('pe', mm(po[:, :], AT[:, :], vS[:, 0, :], True, False), reads=[ATk, vSk], writes=[pok])
                sv = lambda b: sret_d[b, h].rearrange("(c p) v -> p c v", p=128)

                def load(b):
                    S.dma('sp', S0fs[b % 4][0][:], sv(b), writes=S0fs[b % 4][1])

                def prep(b):
                    S0f, S0fk = S0fs[b % 4]
                    S0b, S0bk = S0bs[b % 2]
                    Zb, Zbk = Zbs[b % 2]
                    kdz, kdzk = kdzs[b % 2]
                    S.op('act', lambda e: e.activation(S0b[:].rearrange("p c v -> p (c v)"), S0f[:].rearrange("p c v -> p (c v)"), AF.Copy),
                         reads=S0fk, writes=S0bk)
                    for c in range(2):
                        S.op('dve', lambda e, c=c: e.tensor_tensor(Zb[:, c, :], qdT[:, c, 0:128], colmask[:, 120 - 8 * b:248 - 8 * b], ALU.mult),
                             reads=[qdTk, 'cstb'], writes=Zbk)
                    S.op('dve', lambda e: e.tensor_scalar(kdz[:, :], kd[:, :], rowmask[:, b:b + 1], None, ALU.mult),
                         reads=[kdk, 'cst'], writes=kdzk)
                for b in range(3):
                    load(b)
                prep(0)
                for b in range(NS):
                    S0f, S0fk = S0fs[b % 4]
                    S0b, S0bk = S0bs[b % 2]
                    Zb, Zbk = Zbs[b % 2]
                    kdz, kdzk = kdzs[b % 2]
                    if b + 3 < NS:
                        load(b + 3)
                    pns = []
                    for c in range(2):
                        pn, pnk = ps_next()
                        S.op('pe', mm(pn[:, :], kdz[:, c * 128:(c + 1) * 128], vS[:, 0, :], True, True), reads=[kdzk, vSk], writes=[pnk])
                        pns.append((pn, pnk))
                    for c in range(2):
                        S.op('pe', mm(po[:, :], Zb[:, c, :], S0b[:, c, :], False, (b == NS - 1 and c == 1)),
                             reads=[Zbk, S0bk], writes=[pok])
                    if b + 1 < NS:
                        prep(b + 1)
                    for c in range(2):
                        pn, pnk = pns[c]
                        S.op('dve', lambda e, c=c, pn=pn: e.scalar_tensor_tensor(S0f[:, c, :], S0f[:, c, :], cdecS, pn[:, :], ALU.mult, ALU.add),
                             reads=[pnk, S0fk], writes=S0fk)
                    S.dma('pool', rss_d[b, h].rearrange("(c p) v -> p c v", p=128), S0f[:], reads=S0fk)
            c_['po'], c_['pok'] = po, pok

        def st2o(t, s_):
            if t == 4:
                return
            chunk = t * 4 + s_
            cs = slice(s_ * 128, (s_ + 1) * 128)
            c_ = CH[(t, s_)]
            AT, ATk = c_['AT'], c_['ATk']
            po, pok = c_['po'], c_['pok']
            fns = [mm(po[:, :], AT[:, :], vS[:, s_, :], True, chunk == 0)]
            rd = [ATk, vSk]
            if chunk > 0:
                fns += [mm(po[:, :], qdT[:, c, cs], Sb[:, c, :], False, c == 1) for c in range(2)]
                rd += [qdTk, Sbk]
            S.group('pe', fns, reads=rd, writes=[pok])
            for c in range(2):
                pn, pnk = c_['pn'][c]
                if chunk == 0:
                    S.op('act', lambda e, c=c, pn=pn: e.activation(Sf[:, c, :], pn[:, :], AF.Copy), reads=[pnk], writes=Sfk)
                else:
                    S.op('dve', lambda e, c=c, pn=pn: e.scalar_tensor_tensor(Sf[:, c, :], Sf[:, c, :], cdecP, pn[:, :], ALU.mult, ALU.add),
                         reads=[pnk, Sfk], writes=Sfk)
            if chunk == 15:
                S.dma('sp', rsp_d[h].rearrange("(c p) v -> p c v", p=128), Sf[:], reads=Sfk)
            else:
                S.op('act', lambda e: e.activation(Sb[:].rearrange("p c v -> p (c v)"), Sf[:].rearrange("p c v -> p (c v)"), AF.Copy),
                     reads=Sfk, writes=Sbk)

        def c1(t, s_):
            c_ = CH[(t, s_)]
            po, pok, i2 = c_['po'], c_['pok'], c_['i2']
            st, stk = sts[i2]
            S.op('dve', lambda e: e.bn_stats(st[:, 0:6], po[:, :]), reads=[pok], writes=stk)
            S.op('dve', lambda e: e.bn_aggr(st[:, 6:8], st[:, 0:6]), reads=stk, writes=stk)
            S.op('act', lambda e: e.activation(st[:, 8:9], st[:, 7:8], AF.Sqrt, bias=gepsc[:, 0:1]), reads=stk + ['cst'], writes=stk)

        def c2(t, s_):
            c_ = CH[(t, s_)]
            po, pok, i2 = c_['po'], c_['pok'], c_['i2']
            st, stk = sts[i2]
            on, onk = ons[i2]
            S.op('dve', lambda e: e.reciprocal(st[:, 8:9], st[:, 8:9]), reads=stk, writes=stk)
            S.op('dve', lambda e: e.scalar_tensor_tensor(st[:, 9:10], st[:, 6:7], -1.0, st[:, 8:9], ALU.mult, ALU.mult), reads=stk, writes=stk)
            S.op('act', lambda e: e.activation(on[:, :], po[:, :], AF.Identity, bias=st[:, 9:10], scale=st[:, 8:9]),
                 reads=[pok] + stk, writes=onk)

        def c3(t, s_):
            cs = slice(s_ * 128, (s_ + 1) * 128)
            c_ = CH[(t, s_)]
            i2 = c_['i2']
            on, onk = ons[i2]
            og, ogk = ogs[i2]
            S.op('dve', lambda e: e.tensor_tensor(og[:, :], on[:, :], sg[:, s_, :], ALU.mult), reads=onk + sgk, writes=ogk)
            S.group('pe', [tr(psbf[:, 512 + j * 128:512 + (j + 1) * 128], og[:, j * 128:(j + 1) * 128], identb) for j in range(4)],
                    reads=ogk + ['cstb'], writes=[PSBK])
            S.op('act', lambda e: e.activation(ogT[:, :, cs], psbf[:, 512:1024].rearrange("p (j n) -> p j n", j=4), AF.Copy),
                 reads=[PSBK], writes=ogTk)

        def projC(t):
            t0, n = TILES[t]
            for dc in range(KC):
                pt, pk = ps_next()
                S.group('pe', [mm(pt[:, 0:n], wo[:, j, dc * 128:(dc + 1) * 128], ogT[:, j, 0:n], j == 0, j == 3) for j in range(4)],
                        reads=[wok] + ogTk, writes=[pk])
                resid_add(pt, pk, dc, t, t0, n)

        ps_set(4)
        projA(0)
        projA2(0)
        for t, (t0, n) in enumerate(TILES):
            nsub = n // 128
            st1(t, 0)
            for k in range(nsub + 2):
                if k < nsub:
                    st2(t, k)
                if k + 1 < nsub:
                    st1(t, k + 1)
                if k < nsub:
                    st2o(t, k)
                    c1(t, k)
                if 0 <= k - 1 < nsub:
                    c2(t, k - 1)
                if k == nsub and t + 1 < len(TILES):
                    projA(t + 1)
                if 0 <= k - 2 < nsub:
                    c3(t, k - 2)
            if t + 1 < len(TILES):
                projA2(t + 1)
            projC(t)
            if nn is not None:
                hoist_norm(nn, t, ntmp)
        ps_set(6)

    def attention(li, pre_norm=True, nn=None):
        if pre_norm:
            norm_all(V_XA + 8 * li)
        (KT, KTk), (Vb, Vbk), keep = memkv(li)
        AR.reset(keep)
        wq = [wload2(KC, 512, [(ALL, kpn(wq_d[li][:, hf * 512:(hf + 1) * 512]))]) for hf in range(2)]
        wo = [wload2(KC, 512, [(ALL, kpn(wo_d[li][:, hf * 512:(hf + 1) * 512]))]) for hf in range(2)]
        qx, qxk = AR.alloc('qx', [128, KC, 512], BF16)
        oxT, oxTk = AR.alloc('oxT', [128, KC, 512], BF16)
        ETa, ETak = AR.alloc('ETa', [128, 4, 2, 128], BF16)
        rdS, rdSk = AR.alloc('rdS', [128, 4, 128], F32)
        mk_ = AR.mark()
        ETs = [AR.alloc('ET', [128, 2, 512], BF16) for _ in range(2)]
        rdens = [AR.alloc('rden', [128, 512], F32) for _ in range(2)]
        AR.reset(mk_)
        AR.alloc('pad', [128, 2048], F32)
        Kfs = [AR.alloc('Kf', [128, 2, D], F32) for _ in range(2)]
        ntmp = None
        if nn is not None:
            AR.reset(mk_ + 8192)
            ntmp = norm_tmp(512, 1)
        AR.reset(mk_)
        Vss = [AR.alloc('Vs', [128, 2, D], BF16) for _ in range(2)]
        AR.reset(0)
        KTss = [AR.alloc('KTs', [128, KC, NMEM], BF16) for _ in range(2)]

        def qproj(t, t0, n):
            for cc in range(KC):
                w, wk_ = wq[cc // 4]
                pt, pk = ps_next()
                S.group('pe', [mm(pt[:, 0:n], w[:, kc, (cc % 4) * 128:(cc % 4 + 1) * 128], hT[:, kc, t0:t0 + n], kc == 0, kc == KC - 1)
                               for kc in range(KC)], reads=[hk(t), wk_], writes=[pk])
                S.op('act', lambda e, cc=cc, pt=pt: e.activation(qx[:, cc, 0:n], pt[:, 0:n], AF.Copy), reads=[pk], writes=qxk)

        def oproj(t, t0, n):
            for dc in range(KC):
                w, wk_ = wo[dc // 4]
                pt, pk = ps_next()
                S.group('pe', [mm(pt[:, 0:n], w[:, cc, (dc % 4) * 128:(dc % 4 + 1) * 128], oxT[:, cc, 0:n], cc == 0, cc == KC - 1)
                               for cc in range(KC)], reads=[wk_] + oxTk, writes=[pk])
                resid_add(pt, pk, dc, t, t0, n)

        def scoresH(t, hd):
            t0, n = TILES[t]
            et, etk = ETs[hd % 2]
            for mc in range(2):
                pt, pk = ps_next()
                S.group('pe', [mm(pt[:, 0:n], KT[:, 2 * hd + c, mc * 128:(mc + 1) * 128], qx[:, 2 * hd + c, 0:n], c == 0, c == 1)
                               for c in range(2)], reads=KTk + qxk, writes=[pk])
                S.op('act', lambda e, mc=mc, pt=pt: e.activation(et[:, mc, 0:n], pt[:, 0:n], AF.Exp, scale=1.0 / 16.0),
                     reads=[pk], writes=etk)

        def pvH(t, hd):
            t0, n = TILES[t]
            et, etk = ETs[hd % 2]
            rd, rdk = rdens[hd % 2]
            pd, pdk = ps_next()
            S.group('pe', [mm(pd[:, 0:n], ones1, et[:, mc, 0:n], mc == 0, mc == 1) for mc in range(2)],
                    reads=etk + ['cstb'], writes=[pdk])
            S.op('act', lambda e: e.activation(rd[:, 0:n], pd[:, 0:n], AF.Ln), reads=[pdk], writes=rdk)
            S.op('act', lambda e: e.activation(rd[:, 0:n], rd[:, 0:n], AF.Exp, scale=-1.0), reads=rdk, writes=rdk)
            for c in range(2):
                po, pok = ps_next()
                S.group('pe', [mm(po[:, 0:n], Vb[:, mc, (2 * hd + c) * 128:(2 * hd + c + 1) * 128], et[:, mc, 0:n], mc == 0, mc == 1)
                               for mc in range(2)], reads=Vbk + etk, writes=[pok])
                S.op('dve', lambda e, c=c, po=po: e.tensor_tensor(oxT[:, 2 * hd + c, 0:n], po[:, 0:n], rd[:, 0:n], ALU.mult),
                     reads=[pok] + rdk, writes=oxTk)

        ps_set(6)
        qproj(0, *TILES[0])
        for t, (t0, n) in enumerate(TILES[:4]):
            scoresH(t, 0)
            for hd in range(4):
                if hd + 1 < 4:
                    scoresH(t, hd + 1)
                pvH(t, hd)
            qproj(t + 1, *TILES[t + 1])
            oproj(t, t0, n)
            if nn is not None:
                hoist_norm(nn, t, ntmp)

        t, (t0, n) = 4, TILES[4]
        ps_set(4)
        kview = lambda b: cmk_d[li, b].rearrange("(m p) f -> p m f", p=128)
        vview = lambda b: cmv_d[li, b].rearrange("(m p) f -> p m f", p=128)
        S.dma('sp', Kfs[0][0][:], kview(0), writes=Kfs[0][1])
        for b in range(2):
            S.dma('pool', Vss[b][0][:], vview(b), writes=Vss[b][1])
        for b in range(NS):
            Kf, Kfk = Kfs[b % 2]
            KTs, KTsk = KTss[b % 2]
            if b + 1 < NS:
                S.dma('sp', Kfs[(b + 1) % 2][0][:], kview(b + 1), writes=Kfs[(b + 1) % 2][1])
            for mc in range(2):
                for half in range(2):
                    ptt, ptk = ps_next()
                    S.group('pe', [tr(ptt[:, j * 128:(j + 1) * 128], Kf[:, mc, (half * 4 + j) * 128:(half * 4 + j + 1) * 128], ident)
                                   for j in range(4)], reads=Kfk + ['cst'], writes=[ptk])
                    S.op('act', lambda e, mc=mc, half=half, ptt=ptt: e.activation(
                        KTs[:, half * 4:half * 4 + 4, mc * 128:(mc + 1) * 128], ptt[:].rearrange("p (j n) -> p j n", j=4), AF.Copy),
                        reads=[ptk], writes=KTsk)
            for hd in range(4):
                pl, plk = PSL[hd // 2]
                for mc in range(2):
                    co = ((hd % 2) * 2 + mc) * 128 + 8 * b
                    S.group('pe', [mm(pl[:, co:co + 8], KTs[:, 2 * hd + c, mc * 128:(mc + 1) * 128], qx[:, 2 * hd + c, 8 * b:8 * b + 8], c == 0, c == 1)
                                   for c in range(2)], reads=KTsk + qxk, writes=[plk])
        for k in range(2):
            pl, plk = PSL[k]
            S.op('act', lambda e, k=k, pl=pl: e.activation(ETa[:, 2 * k:2 * k + 2, :, :].rearrange("p h m n -> p (h m n)"), pl[:, :], AF.Exp, scale=1.0 / 16.0),
                 reads=[plk], writes=ETak)
        pd, pdk = ps_next()
        for hd in range(4):
            S.group('pe', [mm(pd[:, hd * 128:(hd + 1) * 128], ones1, ETa[:, hd, mc, :], mc == 0, mc == 1) for mc in range(2)],
                    reads=ETak + ['cstb'], writes=[pdk])
        S.op('act', lambda e: e.activation(rdS[:].rearrange("p h n -> p (h n)"), pd[:, :], AF.Ln), reads=[pdk], writes=rdSk)
        S.op('act', lambda e: e.activation(rdS[:].rearrange("p h n -> p (h n)"), rdS[:].rearrange("p h n -> p (h n)"), AF.Exp, scale=-1.0),
             reads=rdSk, writes=rdSk)
        for b in range(NS):
            Vs, Vsk = Vss[b % 2]
            for cc in range(KC):
                pl, plk = PSL[cc // 4]
                co = (cc % 4) * 128 + 8 * b
                S.group('pe', [mm(pl[:, co:co + 8], Vs[:, mc, cc * 128:(cc + 1) * 128], ETa[:, cc // 2, mc, 8 * b:8 * b + 8], mc == 0, mc == 1)
                               for mc in range(2)], reads=Vsk + ETak, writes=[plk])
            if b + 2 < NS:
                S.dma('pool', Vs[:], vview(b + 2), writes=Vsk)
        for hd in range(4):
            pl, plk = PSL[hd // 2]
            co = ((2 * hd) % 4) * 128
            S.op('dve', lambda e, hd=hd, pl=pl, co=co: e.tensor_tensor(
                oxT[:, 2 * hd:2 * hd + 2, 0:128], pl[:, co:co + 256].rearrange("p (c n) -> p c n", c=2),
                rdS[:, hd, :].unsqueeze(1).to_broadcast([128, 2, 128]), ALU.mult),
                reads=[plk] + rdSk, writes=oxTk)
        oproj(t, t0, n)
        if nn is not None:
            hoist_norm(nn, t, ntmp)
        ps_set(6)

    def ffn(li, pre_norm=True, nn=None):
        if pre_norm:
            norm_all(V_FFN + 8 * li)
        AR.reset()
        convcT, convcTk = AR.alloc('convcT', [128, 2 * NFC, 32], F32)
        haloP, haloPk = AR.alloc('haloP', [128, 2 * NFC, 2], F32)
        hts = [AR.alloc('ht', [128, 2 * NFC, 2], F32) for _ in range(2)]
        htk = [hts[0][1], hts[1][1]]
        ht1, ht1k = AR.alloc('ht1', [128, 2 * NFC], F32)
        cAs = [AR.alloc('cA', [128, 512], F32) for _ in range(2)]
        cGs = [AR.alloc('cG', [128, 512], F32) for _ in range(2)]
        sgs = [AR.alloc('sgt', [128, 512], F32) for _ in range(2)]
        mTs = [AR.alloc('mT', [128, 4, 512], BF16) for _ in range(2)]
        stgs = [AR.alloc('stg', [32, 512], F32) for _ in range(4)]
        ntmp = norm_tmp(512, 1) if nn is not None else None
        cw = lambda j, cc: vec[:, V_CW + (li * 3 + j) * 44 + cc:V_CW + (li * 3 + j) * 44 + cc + 1]
        cb = lambda cc: vec[:, V_CB + li * 44 + cc:V_CB + li * 44 + cc + 1]
        def cache_prep():
            for pc in range(11):
                stg, stgk = stgs[pc % 2]
                S.dma('sp', stg[:], cconv_d[li][:, pc * 512:(pc + 1) * 512], writes=stgk)
                S.group('pe', [tr(PST[:, j * 32:(j + 1) * 32], stg[0:32, j * 128:(j + 1) * 128], ident[0:32, 0:32]) for j in range(4)],
                        reads=stgk + ['cst'], writes=[PSTK])
                S.op('act', lambda e, pc=pc: e.activation(convcT[:, pc * 4:pc * 4 + 4, :], PST[:, 0:128].rearrange("p (j n) -> p j n", j=4), AF.Copy),
                     reads=[PSTK], writes=convcTk)

        kk = [0]

        def conv_e(pt, pk, cc, cbuf, cbk, t, n):
            S.op('act', lambda e: e.activation(cbuf[:, 0:n], pt[:, 0:n], AF.Identity, bias=cb(cc), scale=cw(2, cc)),
                 reads=[pk, 'vec'], writes=cbk)
            if t < 3:
                hb_ = hts[t % 2][0][:, cc, :]
                S.op('act', lambda e: e.activation(ht1[:, cc:cc + 1], pt[:, n - 1:n], AF.Identity, scale=cw(1, cc)),
                     reads=[pk, 'vec'], writes=ht1k)
                S.op('act', lambda e: e.activation(hb_[:, 0:1], pt[:, n - 2:n - 1], AF.Identity, bias=ht1[:, cc:cc + 1], scale=cw(0, cc)),
                     reads=[pk, 'vec'] + ht1k, writes=htk[t % 2])
                S.op('act', lambda e: e.activation(hb_[:, 1:2], pt[:, n - 1:n], AF.Identity, scale=cw(0, cc)),
                     reads=[pk, 'vec'], writes=htk[t % 2])
            elif t == 3:
                S.op('act', lambda e: e.activation(haloP[:, cc, :], pt[:, n - 2:n], AF.Copy), reads=[pk], writes=haloPk)
            return (pt, pk)

        def conv_f(ppk, cc, cbuf, cbk, t, n):
            pt, pk = ppk
            if t < 4:
                S.op('dve', lambda e: e.scalar_tensor_tensor(cbuf[:, 1:n], pt[:, 0:n - 1], cw(1, cc), cbuf[:, 1:n], ALU.mult, ALU.add),
                     reads=[pk, 'vec'] + cbk, writes=cbk)
                S.op('dve', lambda e: e.scalar_tensor_tensor(cbuf[:, 2:n], pt[:, 0:n - 2], cw(0, cc), cbuf[:, 2:n], ALU.mult, ALU.add),
                     reads=[pk, 'vec'] + cbk, writes=cbk)
                if t > 0:
                    S.op('dve', lambda e: e.tensor_tensor(cbuf[:, 0:2], cbuf[:, 0:2], hts[(t - 1) % 2][0][:, cc, :], ALU.add),
                         reads=htk[(t - 1) % 2] + cbk, writes=cbk)
            else:
                c3 = cbuf[:, 0:128].rearrange("p (b t) -> p b t", t=LS)
                p3 = pt[:, 0:128].rearrange("p (b t) -> p b t", t=LS)
                cc3 = convcT[:, cc, :].rearrange("p (b t) -> p b t", t=2)
                S.op('dve', lambda e: e.scalar_tensor_tensor(c3[:, :, 1:8], p3[:, :, 0:7], cw(1, cc), c3[:, :, 1:8], ALU.mult, ALU.add),
                     reads=[pk, 'vec'] + cbk, writes=cbk)
                S.op('dve', lambda e: e.scalar_tensor_tensor(c3[:, :, 2:8], p3[:, :, 0:6], cw(0, cc), c3[:, :, 2:8], ALU.mult, ALU.add),
                     reads=[pk, 'vec'] + cbk, writes=cbk)
                S.op('dve', lambda e: e.scalar_tensor_tensor(c3[:, :, 0:1], cc3[:, :, 1:2], cw(1, cc), c3[:, :, 0:1], ALU.mult, ALU.add),
                     reads=convcTk + ['vec'] + cbk, writes=cbk)
                S.op('dve', lambda e: e.scalar_tensor_tensor(c3[:, :, 0:2], cc3[:, :, 0:2], cw(0, cc), c3[:, :, 0:2], ALU.mult, ALU.add),
                     reads=convcTk + ['vec'] + cbk, writes=cbk)
                S.op('dve', lambda e: e.tensor_copy(cc3, p3[:, :, 6:8]), reads=[pk], writes=convcTk)

        groups = [(0, 2), (2, 4), (6, 4), (10, 4), (14, 4), (18, 4)]
        items = [(gi, t) for gi in range(len(groups)) for t in range(len(TILES))]
        Wg = {}

        def loadw(gi):
            a0, na = groups[gi]
            wa = wload2(KC, 512, [(lambda v: v[:, :, 0:na * 128], kpn(wup_d[li][:, a0 * 128:(a0 + na) * 128]))])
            wg_ = wload2(KC, 512, [(lambda v: v[:, :, 0:na * 128], kpn(wup_d[li][:, DFF + a0 * 128:DFF + (a0 + na) * 128]))])
            wd = wload2(4, 1024, [(lambda v: v[:, 0:na, :], kpn(wdn_d[li][a0 * 128:(a0 + na) * 128, :]))])
            Wg[gi] = (wa, wg_, wd)

        def stA(i):
            gi, t = items[i]
            a0, na = groups[gi]
            t0, n = TILES[t]
            (wa, wak), (wg_, wgk), _ = Wg[gi]
            mt, mtk = mTs[i % 2]
            pend = {}

            def E(j):
                lst = []
                for (w, wkey, cc, (cbuf, cbk)) in ((wa, wak, a0 + j, cAs[j % 2]), (wg_, wgk, NFC + a0 + j, cGs[j % 2])):
                    pt, pk = ps_next()
                    S.group('pe', [mm(pt[:, 0:n], w[:, kc, j * 128:(j + 1) * 128], hT[:, kc, t0:t0 + n], kc == 0, kc == KC - 1)
                                   for kc in range(KC)], reads=[hk(t), wkey], writes=[pk])
                    i4 = conv_e(pt, pk, cc, cbuf, cbk, t, n)
                    lst.append((i4, cc, cbuf, cbk))
                pend[j] = lst

            def F(j):
                for (i4, cc, cbuf, cbk) in pend[j]:
                    conv_f(i4, cc, cbuf, cbk, t, n)
                cA, cAk = cAs[j % 2]
                cG, cGk = cGs[j % 2]
                sgt, sgtk = sgs[j % 2]
                S.op('act', lambda e: e.activation(sgt[:, 0:n], cG[:, 0:n], AF.Silu), reads=cGk, writes=sgtk)
                S.op('pool', lambda e: e.tensor_tensor(mt[:, j, 0:n], cA[:, 0:n], sgt[:, 0:n], ALU.mult), reads=cAk + sgtk, writes=mtk)

            E(0)
            for j in range(na):
                if j + 1 < na:
                    E(j + 1)
                F(j)

        def stB(i):
            gi, t = items[i]
            a0, na = groups[gi]
            t0, n = TILES[t]
            _, _, (wd, wdk) = Wg[gi]
            mt, mtk = mTs[i % 2]
            for dc in range(KC):
                pt, pk = ps_next()
                S.group('pe', [mm(pt[:, 0:n], wd[:, j, dc * 128:(dc + 1) * 128], mt[:, j, 0:n], j == 0, j == na - 1) for j in range(na)],
                        reads=[wdk] + mtk, writes=[pk])
                resid_add(pt, pk, dc, t, t0, n)

        loadw(0)
        loadw(1)
        stA(0)
        for i in range(len(items)):
            if i == 2:
                cache_prep()
            if i + 1 < len(items):
                stA(i + 1)
            stB(i)
            gi, t = items[i]
            if t == len(TILES) - 1 and gi + 2 < len(groups):
                loadw(gi + 2)
            if nn is not None and gi == len(groups) - 1:
                hoist_norm(nn, t, ntmp)
        for pc in range(11):
            stg, stgk = stgs[pc % 4]
            ptt, ptk = ps_next()
            S.group('pe', [tr(ptt[0:2, j * 128:(j + 1) * 128], haloP[:, pc * 4 + j, :], ident) for j in range(4)],
                    reads=haloPk + ['cst'], writes=[ptk])
            S.op('act', lambda e: e.activation(stg[0:2, :], ptt[0:2, :], AF.Copy), reads=[ptk], writes=stgk)
            S.dma('sp', cbp_d[li][:, pc * 512:(pc + 1) * 512], stg[0:2, :], reads=stgk)
        for pc in range(11):
            stg, stgk = stgs[pc % 4]
            ptt, ptk = ps_next()
            S.group('pe', [tr(ptt[0:32, j * 128:(j + 1) * 128], convcT[:, pc * 4 + j, :], ident) for j in range(4)],
                    reads=convcTk + ['cst'], writes=[ptk])
            S.op('act', lambda e: e.activation(stg[0:32, :], ptt[0:32, :], AF.Copy), reads=[ptk], writes=stgk)
            S.dma('sp', cbs_d[li][:, pc * 512:(pc + 1) * 512], stg[0:32, :], reads=stgk)

    def pool_mixer(nn=None):
        AR.reset()
        tmp = norm_tmp(512, 1)
        gcol = V_MIX + 8
        wp, wpk = wload2(KC, 256, [(ALL, pw_d.rearrange("g (c p) d -> p (g c) d", p=128))])
        poolcT, poolcTk = AR.alloc('poolcT', [128, KC, NS * 15], F32)
        haloHe = {'dve': AR.alloc('haloHe', [128, 4, 15], F32), 'pool': AR.alloc('haloHo', [128, 4, 15], F32)}
        hbs = {e: AR.alloc('hb', [128, 15 + 512], F32) for e in ('dve', 'pool')}
        sas = {e: [AR.alloc('sa', [128, 15 + 512], F32) for _ in range(2)] for e in ('dve', 'pool')}
        pT, pTk = AR.alloc('pT', [128, KC, 512], BF16)
        hS, hSk = AR.alloc('hS', [128, KC, 128], F32)
        stgo, stgok = AR.alloc('stgo', [128, D], F32)
        icn, icnk = AR.alloc('icn', [128, 4, 16], F32)
        tmp2 = dict(tmp)
        tmp2['rs'] = [AR.alloc('rstd2', [128, 512], F32)]
        tmp2['i'] = 0
        S.dma('sp', icn[:], icn_d.rearrange("p (g n) -> p g n", g=4)[:, :, 0:16], writes=icnk)
        for pc in range(2):
            S.dma('sp', stgo[0:120, :], cpool_d[pc * 120:(pc + 1) * 120, :], writes=stgok)
            for half in range(2):
                S.group('pe', [tr(PST[:, j * 120:(j + 1) * 120], stgo[0:120, (half * 4 + j) * 128:(half * 4 + j + 1) * 128], ident[0:120, 0:120])
                               for j in range(4)], reads=stgok + ['cst'], writes=[PSTK])
                S.op('act', lambda e, pc=pc, half=half: e.activation(
                    poolcT[:, half * 4:half * 4 + 4, pc * 120:(pc + 1) * 120], PST[:, 0:480].rearrange("p (j n) -> p j n", j=4), AF.Copy),
                    reads=[PSTK], writes=poolcTk)
        stats = {0: norm_stats(xT, [xk(0)], TILES[0][0], TILES[0][1], tmp)}
        for t, (t0, n) in enumerate(TILES):
            rs, rsk = stats[t]
            def chain(kc):
                g = kc // 2
                w = 2 << g
                EA = 'dve' if kc % 2 == 0 else 'pool'
                E_ = 'dve'
                hb, hbk = hbs[EA]
                haloHx, hHk = haloHe[EA]
                pTkk = pTk[kc * 2:kc * 2 + 2]
                gc_ = vec[:, gcol + kc:gcol + kc + 1]

                def mulmul(out, in0, okeys):
                    if E_ == 'dve':
                        S.op('dve', lambda e: e.scalar_tensor_tensor(out, in0, gc_, rs[:, 0:n], ALU.mult, ALU.mult),
                             reads=[xk(t), 'vec'] + rsk, writes=okeys)
                    else:
                        S.op('pool', lambda e: e.tensor_scalar(out, in0, gc_, None, ALU.mult), reads=[xk(t), 'vec'], writes=okeys)
                        S.op('pool', lambda e: e.tensor_tensor(out, out, rs[:, 0:n], ALU.mult), reads=rsk + okeys, writes=okeys)

                def scalesub(out, cu, cuk, hh, okeys):
                    if E_ == 'dve':
                        S.op('dve', lambda e: e.scalar_tensor_tensor(out, cu, 1.0 / w, hh, ALU.mult, ALU.subtract), reads=cuk + hbk, writes=okeys)
                    else:
                        S.op('pool', lambda e: e.tensor_scalar(cu, cu, 1.0 / w, None, ALU.mult), reads=cuk, writes=cuk)
                        S.op('pool', lambda e: e.tensor_tensor(out, cu, hh, ALU.subtract), reads=cuk + hbk, writes=okeys)

                if t < 4:
                    L = 15 + n
                    mulmul(hb[:, 15:L], xT[:, kc, t0:t0 + n], hbk)
                    if t == 0:
                        S.op(E_, lambda e: e.memset(hb[:, 0:15], 0.0), writes=hbk)
                    else:
                        S.op(E_, lambda e: e.tensor_copy(hb[:, 0:15], haloHx[:, kc // 2, :]), reads=hHk, writes=hbk)
                    S.op(E_, lambda e: e.tensor_copy(haloHx[:, kc // 2, :], hb[:, n:n + 15]), reads=hbk, writes=hHk)
                    cur, curk = hb, hbk
                    yield
                    for step in range(g + 1):
                        sh = 1 << step
                        v0 = (2 << step) - 1
                        nx, nxk = sas[EA][step % 2]
                        S.op(EA, lambda e, cur=cur, nx=nx: e.tensor_tensor(nx[:, v0:L], cur[:, v0:L], cur[:, v0 - sh:L - sh], ALU.add),
                             reads=curk, writes=nxk)
                        cur, curk = nx, nxk
                    yield
                    if t == 0:
                        S.op(E_, lambda e: e.tensor_tensor(cur[:, 15:31], cur[:, 15:31], icn[:, g, :], ALU.mult), reads=curk + icnk, writes=curk)
                        S.op(E_, lambda e: e.tensor_tensor(pT[:, kc, 0:16], cur[:, 15:31], hb[:, 15:31], ALU.subtract), reads=curk + hbk, writes=pTkk)
                        scalesub(pT[:, kc, 16:n], cur[:, 31:L], curk, hb[:, 31:L], pTkk)
                    else:
                        scalesub(pT[:, kc, 0:n], cur[:, 15:L], curk, hb[:, 15:L], pTkk)
                else:
                    v3 = lambda a, lo, hi: a[:, 0:NS * 23].rearrange("p (b x) -> p b x", x=23)[:, :, lo:hi]
                    hSkk = hSk[kc:kc + 1]
                    mulmul(hS[:, kc, :], xT[:, kc, t0:t0 + n], hSkk)
                    S.op(E_, lambda e: e.tensor_copy(v3(hb, 15, 23), hS[:, kc, :].rearrange("p (b t) -> p b t", t=LS)), reads=hSkk, writes=hbk)
                    S.op(E_, lambda e: e.tensor_copy(v3(hb, 0, 15), poolcT[:, kc, :].rearrange("p (b t) -> p b t", t=15)),
                         reads=poolcTk, writes=hbk)
                    cur, curk = hb, hbk
                    yield
                    for step in range(g + 1):
                        sh = 1 << step
                        v0 = (2 << step) - 1
                        nx, nxk = sas[EA][step % 2]
                        S.op(EA, lambda e, cur=cur, nx=nx: e.tensor_tensor(v3(nx, v0, 23), v3(cur, v0, 23), v3(cur, v0 - sh, 23 - sh), ALU.add),
                             reads=curk, writes=nxk)
                        cur, curk = nx, nxk
                    yield
                    scalesub(pT[:, kc, 0:128].rearrange("p (b t) -> p b t", t=LS), v3(cur, 15, 23), curk, v3(hb, 15, 23), pTkk)
            for p_ in range(KC // 2):
                ge, go = chain(2 * p_), chain(2 * p_ + 1)
                next(go)
                next(go)
                for _ in ge:
                    pass
                for _ in go:
                    pass
            if t + 1 < len(TILES):
                stats[t + 1] = norm_stats(xT, [xk(t + 1)], TILES[t + 1][0], TILES[t + 1][1], tmp)
            haloHk = haloHe['dve'][1] + haloHe['pool'][1]
            for g in range(4):
                for dd in range(2):
                    pt, pk = ps_next()
                    S.group('pe', [mm(pt[:, 0:n], wp[:, g * 2 + c, dd * 128:(dd + 1) * 128], pT[:, 2 * g + c, 0:n], c == 0, c == 1)
                                   for c in range(2)], reads=[wpk] + pTk, writes=[pk])
                    resid_add(pt, pk, 2 * g + dd, t, t0, n, scol=vec[:, V_PSC + 2 * g + dd:V_PSC + 2 * g + dd + 1])
            if nn is not None:
                hoist_norm(nn, t, tmp2)
            if t == 3:
                for half in range(2):
                    S.group('pe', [tr(PST[0:15, j * 128:(j + 1) * 128],
                                      haloHe['dve' if (half * 4 + j) % 2 == 0 else 'pool'][0][:, (half * 4 + j) // 2, :], ident) for j in range(4)],
                            reads=haloHk + ['cst'], writes=[PSTK])
                    S.op('act', lambda e, half=half: e.activation(stgo[0:15, half * 512:(half + 1) * 512], PST[0:15, :], AF.Copy),
                         reads=[PSTK], writes=stgok)
                S.dma('sp', pbp_d, stgo[0:15, :], reads=stgok)
        S.dma('sp', pbs_d[:, 0:7, :], cpool_d.rearrange("(b r) f -> b r f", r=15)[:, 8:15, :])
        for half in range(2):
            S.group('pe', [tr(PST[:, j * 128:(j + 1) * 128], hS[:, half * 4 + j, :], ident) for j in range(4)],
                    reads=hSk + ['cst'], writes=[PSTK])
            S.op('act', lambda e, half=half: e.activation(stgo[:, half * 512:(half + 1) * 512], PST[:, :], AF.Copy), reads=[PSTK], writes=stgok)
        for b in range(NS):
            S.dma('sp', pbs_d[b, 7:15, :], stgo[8 * b:8 * b + 8, :], reads=stgok)

    def pool_mixer_pe(nn=None):
        AR.reset()
        gcol = V_MIX + 8
        wp, wpk = wload2(KC, 256, [(ALL, pw_d.rearrange("g (c p) d -> p (g c) d", p=128))])
        pm, pmk = AR.alloc('pm', [128, 24, 128], BF16)
        S.dma('pool', pm[:].rearrange("p a b -> p (a b)"), pmat_d, writes=pmk)
        PM = lambda g, k: pm[:, g * 6 + k, :]
        tmpS = norm_tmp(128, 1)
        tmp2 = norm_tmp(512, 1) if nn is not None else None
        poolcT, poolcTk = AR.alloc('poolcT', [128, KC, NS * 15], BF16)
        hS, hSk = AR.alloc('hS', [128, KC, 128], F32)
        hP, hPk = AR.alloc('hP', [128, KC, 16], F32)
        stgo, stgok = AR.alloc('stgo', [128, D], F32)
        stgc = [AR.alloc('stgc', [128, D], F32)] * 2
        zs = [AR.alloc('z', [128, D], BF16) for _ in range(5)]
        zc = [AR.alloc('zc', [128, D], BF16) for _ in range(2)]
        for pc in range(2):
            S.dma('sp', stgc[pc][0][0:120, :], cpool_d[pc * 120:(pc + 1) * 120, :], writes=stgc[pc][1])
            for half in range(2):
                ptt, ptk = ps_next()
                S.group('pe', [tr(ptt[:, j * 120:(j + 1) * 120], stgc[pc][0][0:120, (half * 4 + j) * 128:(half * 4 + j + 1) * 128], ident[0:120, 0:120])
                               for j in range(4)], reads=stgc[pc][1] + ['cst'], writes=[ptk])
                S.op('act', lambda e, pc=pc, half=half, ptt=ptt: e.activation(
                    poolcT[:, half * 4:half * 4 + 4, pc * 120:(pc + 1) * 120], ptt[:, 0:480].rearrange("p (j n) -> p j n", j=4), AF.Copy),
                    reads=[ptk], writes=poolcTk)

        rs, rsk = norm_stats(xT, [xk(3)], SEQ - 16, 16, tmpS)
        for kc in range(KC):
            S.op('dve', lambda e, kc=kc: e.scalar_tensor_tensor(hP[:, kc, :], xT[:, kc, SEQ - 16:SEQ], vec[:, gcol + kc:gcol + kc + 1], rs[:, 0:16],
                                                             ALU.mult, ALU.mult), reads=[xk(3), 'vec'] + rsk, writes=hPk)
        rs, rsk = norm_stats(xT, [xk(4)], SEQ, 128, tmpS)
        for kc in range(KC):
            S.op('dve', lambda e, kc=kc: e.scalar_tensor_tensor(hS[:, kc, :], xT[:, kc, SEQ:SEQ + 128], vec[:, gcol + kc:gcol + kc + 1], rs[:, 0:128],
                                                             ALU.mult, ALU.mult), reads=[xk(4), 'vec'] + rsk, writes=hSk)
        for half in range(2):
            ptt, ptk = ps_next()
            S.group('pe', [tr(ptt[0:15, j * 128:(j + 1) * 128], hP[:, half * 4 + j, 1:16], ident) for j in range(4)],
                    reads=hPk + ['cst'], writes=[ptk])
            S.op('act', lambda e, half=half, ptt=ptt: e.activation(stgo[0:15, half * 512:(half + 1) * 512], ptt[0:15, :], AF.Copy),
                 reads=[ptk], writes=stgok)
        S.dma('sp', pbp_d, stgo[0:15, :], reads=stgok)
        S.dma('sp', pbs_d[:, 0:7, :], cpool_d.rearrange("(b r) f -> b r f", r=15)[:, 8:15, :])
        for half in range(2):
            ptt, ptk = ps_next()
            S.group('pe', [tr(ptt[:, j * 128:(j + 1) * 128], hS[:, half * 4 + j, :], ident) for j in range(4)],
                    reads=hSk + ['cst'], writes=[ptk])
            S.op('act', lambda e, half=half, ptt=ptt: e.activation(stgo[:, half * 512:(half + 1) * 512], ptt[:, :], AF.Copy), reads=[ptk], writes=stgok)
        for b in range(NS):
            S.dma('sp', pbs_d[b, 7:15, :], stgo[8 * b:8 * b + 8, :], reads=stgok)
        def zproj(lhs_fn, rows, zt, ztk, rkeys):
            for hf in range(2):
                ptt, ptk = ps_next()
                for gg in range(2):
                    g = hf * 2 + gg
                    S.group('pe', [mm(ptt[0:rows, gg * 256:(gg + 1) * 256], lhs_fn(2 * g + cc), wp[:, g * 2 + cc, :], cc == 0, cc == 1)
                                   for cc in range(2)], reads=rkeys + [wpk], writes=[ptk])
                S.op('act', lambda e, hf=hf, ptt=ptt: e.activation(zt[0:rows, hf * 512:(hf + 1) * 512], ptt[0:rows, :], AF.Copy),
                     reads=[ptk], writes=ztk)

        for pc in range(2):
            zproj(lambda kc, pc=pc: poolcT[:, kc, pc * 120:(pc + 1) * 120], 120, zc[pc][0], zc[pc][1], poolcTk)
        zi = 0
        prev = None
        for t, (t0, n) in enumerate(TILES[:4]):
            cur = []
            for sub in range(4):
                zt, ztk = zs[zi % 5]
                zi += 1
                c0 = t0 + sub * 128
                zproj(lambda kc, c0=c0: hT[:, kc, c0:c0 + 128], 128, zt, ztk, [hk(t)])
                cur.append((zt, ztk))
            for g in range(4):
                for dd in range(2):
                    fs = slice(g * 256 + dd * 128, g * 256 + (dd + 1) * 128)
                    pt, pk = ps_next()
                    for sub in range(4):
                        zt, ztk = cur[sub]
                        first = (t == 0 and sub == 0)
                        pz = prev if sub == 0 else cur[sub - 1]
                        fns = [mm(pt[:, sub * 128:(sub + 1) * 128], zt[:, fs], PM(g, 2 if first else 0), True, first)]
                        rd = ztk + pmk
                        if not first:
                            fns.append(mm(pt[:, sub * 128:(sub + 1) * 128], pz[0][:, fs], PM(g, 1), False, True))
                            rd = rd + pz[1]
                        S.group('pe', fns, reads=rd, writes=[pk])
                    resid_add(pt, pk, 2 * g + dd, t, t0, n, scol=vec[:, V_PSC + 2 * g + dd:V_PSC + 2 * g + dd + 1])
            prev = cur[3]
            if nn is not None:
                hoist_norm(nn, t, tmp2)
        t, (t0, n) = 4, TILES[4]
        zt, ztk = zs[zi % 5]
        zproj(lambda kc: hT[:, kc, t0:t0 + 128], 128, zt, ztk, [hk(t)])
        for g in range(4):
            for dd in range(2):
                fs = slice(g * 256 + dd * 128, g * 256 + (dd + 1) * 128)
                pt, pk = ps_next()
                S.group('pe', [mm(pt[:, 0:128], zt[:, fs], PM(g, 3), True, False),
                               mm(pt[:, 0:128], zc[0][0][0:120, fs], pm[0:120, g * 6 + 4, :], False, False),
                               mm(pt[:, 0:128], zc[1][0][0:120, fs], pm[0:120, g * 6 + 5, :], False, True)],
                        reads=ztk + zc[0][1] + zc[1][1] + pmk, writes=[pk])
                resid_add(pt, pk, 2 * g + dd, t, t0, n, scol=vec[:, V_PSC + 2 * g + dd:V_PSC + 2 * g + dd + 1])
        if nn is not None:
            hoist_norm(nn, t, tmp2)

    def final():
        AR.reset()
        tmp = norm_tmp(512, 2)
        yTs = [AR.alloc('yT', [128, KC, 128], F32) for _ in range(2)]
        yos = [AR.alloc('yo', [128, D], F32) for _ in range(3)]
        stats = {0: norm_stats(xT, [xk(0)], TILES[0][0], TILES[0][1], tmp)}
        i = 0
        for t, (t0, n) in enumerate(TILES):
            if t + 1 < len(TILES):
                stats[t + 1] = norm_stats(xT, [xk(t + 1)], TILES[t + 1][0], TILES[t + 1][1], tmp)
            rs, rsk = stats[t]
            for sub in range(n // 128):
                c0 = t0 + sub * 128
                yT, yTk = yTs[i % 2]
                yo, yok = yos[i % 3]
                i += 1
                for kc in range(KC):
                    S.op('dve', lambda e, kc=kc: e.scalar_tensor_tensor(
                        yT[:, kc, :], xT[:, kc, c0:c0 + 128], vec[:, V_FIN + kc:V_FIN + kc + 1], rs[:, sub * 128:(sub + 1) * 128],
                        ALU.mult, ALU.mult), reads=[xk(t), 'vec'] + rsk, writes=yTk)
                for half in range(2):
                    ptt, ptk = ps_next()
                    S.group('pe', [tr(ptt[:, j * 128:(j + 1) * 128], yT[:, half * 4 + j, :], ident) for j in range(4)],
                            reads=yTk + ['cst'], writes=[ptk])
                    S.op('act', lambda e, half=half, ptt=ptt: e.activation(yo[:, half * 512:(half + 1) * 512], ptt[:, :], AF.Copy), reads=[ptk], writes=yok)
                S.dma('sp', y_d[c0:c0 + 128, :] if t < 4 else ys_d, yo[:], reads=yok)

    import os
    PH = os.environ.get("KPH", "all")
    io_in(nn=V_MIX)
    if PH in ("all", "ret", "l0"):
        for h in range(H):
            retention_head(h, nn=(V_XA if h == H - 1 else None))
    if PH in ("all", "l0"):
        attention(0, pre_norm=False, nn=V_FFN)
        ffn(0, pre_norm=False, nn=V_MIX + 8)
    if PH in ("all",):
        pool_mixer_pe(nn=V_XA + 8)
        attention(1, pre_norm=False, nn=V_FFN + 8)
        ffn(1, pre_norm=False)
        final()
    if PH == "mem":
        memkv(0)
        memkv(1)
    S.finish()
    print("instructions:", S.nins, "waits:", S.nwait, "sems:", S.nsem, flush=True)
    return nc


def _host_consts():
    c = {}
    cst = np.zeros((128, 128 + 16 + 2 + 248 + 256), np.float32)
    cst[:, 0:128] = np.eye(128, dtype=np.float32)
    for b in range(NS):
        cst[b * 8:(b + 1) * 8, 128 + b] = 1.0
    cst[:, 144] = EPS
    cst[:, 145] = GN_EPS
    cst[:, 146 + 120:146 + 128] = 1.0
    cst[:, 394:522] = 1.0 / D
    cst[:, 522:650] = 1.0
    c['cst'] = cst
    inv = (1.0 / (np.float32(10000.0) ** (np.arange(0, DK, 2, dtype=np.float32) / np.float32(DK)))).astype(np.float32)
    pos = np.concatenate([np.arange(SEQ, dtype=np.float32),
                          np.tile(np.float32(PAST) + np.arange(LS, dtype=np.float32), NS)]).astype(np.float32)
    ang = (pos[:, None] * inv[None, :]).astype(np.float32)
    c['cosT'] = np.ascontiguousarray(np.cos(ang).T.astype(np.float32))
    c['sinT'] = np.ascontiguousarray(np.sin(ang).T.astype(np.float32))
    lg = np.log(1.0 - 2.0 ** (-5.0 - np.arange(H, dtype=np.float64)))
    hcf = np.zeros((H, 128, 258), np.float32)
    hcq = np.zeros((H, 128, 640), np.float32)
    i128 = np.arange(128)
    for h in range(H):
        k = i128[:, None]
        q = i128[None, :]
        mp = np.where(q >= k, np.exp((q - k) * lg[h]), 0.0) * (DK ** -0.5)
        ms = np.where((q >= k) & (q // LS == k // LS), np.exp((q - k) * lg[h]), 0.0) * (DK ** -0.5)
        hcf[h, :, 0:128] = mp
        hcf[h, :, 128:256] = ms
        hcf[h, :, 256] = np.exp((127 - i128) * lg[h]) * (DK ** -0.5)
        hcf[h, :, 257] = np.exp((LS - 1 - (i128 % LS)) * lg[h]) * (DK ** -0.5)
        hcq[h, :, 0:512] = np.exp(((np.arange(512) % 128) + 1) * lg[h])[None, :]
        hcq[h, :, 512:640] = np.exp(((np.arange(128) % LS) + 1) * lg[h])[None, :]
    c['hcf'] = hcf
    c['hcq'] = hcq
    icn = np.zeros((128, 4 * 512), np.float32)
    for g, w in enumerate((2, 4, 8, 16)):
        icn[:, g * 512:(g + 1) * 512] = (1.0 / np.minimum(np.arange(512) + 1.0, float(w)))[None, :]
    c['icn'] = icn
    pmat = np.zeros((128, 24, 128), np.float32)
    i128 = np.arange(128)
    for g, w in enumerate((2, 4, 8, 16)):
        kp = i128[:, None]
        p = i128[None, :]
        win = ((kp <= p) & (kp > p - w)).astype(np.float64)
        eye = (kp == p).astype(np.float64)
        pmat[:, g * 6 + 0, :] = win / w - eye
        pmat[:, g * 6 + 1, :] = ((kp - 128 > p - w)).astype(np.float64) / w
        pmat[:, g * 6 + 2, :] = win / np.minimum(p + 1.0, float(w)) - eye
        sb = (kp // LS == p // LS)
        pmat[:, g * 6 + 3, :] = np.where(sb, win / w - eye, 0.0)
        for half in range(2):
            r = i128[:, None]
            bq = r // 15 + 8 * half
            r15 = r % 15
            t_ = p % LS
            ok = (r < 120) & (bq == p // LS) & (r15 > 15 + t_ - w)
            pmat[:, g * 6 + 4 + half, :] = np.where(ok, 1.0 / w, 0.0)
    c['pmat'] = pmat.reshape(128, 24 * 128)
    return c


def _cols(v):
    return np.ascontiguousarray(v.reshape(-1, 128).T)


_CACHE = {}


def kernel(x_prompt, x_sample, mem_prompt, cache_mem_k, cache_mem_v, state_ret, cache_pool, cache_ffn_conv,
           w_ret_in, ret_gn, w_ret_out, pool_w, pool_scale, norm_mem, w_xq, w_xk, w_xv, w_xo,
           w_up, conv_w, conv_b, w_down, norm_mix, norm_xattn, norm_ffn, norm_final):
    f = lambda a: np.ascontiguousarray(np.asarray(a, dtype=np.float32))
    x_prompt, x_sample, mem_prompt = f(x_prompt), f(x_sample), f(mem_prompt)
    cache_mem_k, cache_mem_v, state_ret = f(cache_mem_k), f(cache_mem_v), f(state_ret)
    cache_pool, cache_ffn_conv = f(cache_pool), f(cache_ffn_conv)
    if 'nc' not in _CACHE:
        _CACHE['nc'] = build_program()
        _CACHE['c'] = _host_consts()
    nc = _CACHE['nc']
    cc = _CACHE['c']
    vec = np.zeros((128, NVEC), np.float32)
    for i in range(2):
        vec[:, V_MIX + 8 * i:V_MIX + 8 * i + 8] = _cols(f(norm_mix)[i])
        vec[:, V_XA + 8 * i:V_XA + 8 * i + 8] = _cols(f(norm_xattn)[i])
        vec[:, V_FFN + 8 * i:V_FFN + 8 * i + 8] = _cols(f(norm_ffn)[i])
        vec[:, V_MEM + 8 * i:V_MEM + 8 * i + 8] = _cols(f(norm_mem)[i])
        for j in range(3):
            vec[:, V_CW + (i * 3 + j) * 44:V_CW + (i * 3 + j + 1) * 44] = _cols(f(conv_w)[i, j])
        vec[:, V_CB + i * 44:V_CB + (i + 1) * 44] = _cols(f(conv_b)[i])
    vec[:, V_FIN:V_FIN + 8] = _cols(f(norm_final))
    vec[:, V_PSC:V_PSC + 8] = _cols(f(pool_scale)[0])
    shared = {
        "w_in": f(w_ret_in)[0], "gn": f(ret_gn).reshape(1, 2048), "w_out": f(w_ret_out)[0], "pw": f(pool_w)[0],
        "wq": f(w_xq), "wk": f(w_xk), "wv": f(w_xv), "wo": f(w_xo), "wup": f(w_up), "wdn": f(w_down),
        "vec": vec, "cst": cc['cst'], "cosT": cc['cosT'], "sinT": cc['sinT'], "hcf": cc['hcf'], "hcq": cc['hcq'],
        "icn": cc['icn'], "pmat": cc['pmat'],
    }
    in_maps = []
    for c in range(NCORES):
        sl = slice(c * NS, (c + 1) * NS)
        m = dict(shared)
        m["x"] = x_prompt[c]
        m["xs"] = x_sample[sl].reshape(NS * LS, D)
        m["mem"] = mem_prompt[c]
        m["cmk"] = np.ascontiguousarray(cache_mem_k[:, sl].reshape(2, NS, NMEM, D))
        m["cmv"] = np.ascontiguousarray(cache_mem_v[:, sl].reshape(2, NS, NMEM, D))
        m["sret"] = np.ascontiguousarray(state_ret[0, sl])
        m["cpool"] = np.ascontiguousarray(cache_pool[0, sl].reshape(NS * 15, D))
        m["cconv"] = np.ascontiguousarray(cache_ffn_conv[:, sl].reshape(2, NS * 2, 2 * DFF))
        in_maps.append(m)
    res = run_bass_kernel_spmd(nc, in_maps, core_ids=list(range(NCORES)))
    R = res.results
    st = lambda k: np.stack([np.asarray(R[c][k], dtype=np.float32) for c in range(NCORES)])
    y_prompt = st("y")
    y_sample = st("ys").reshape(NCORES * NS, LS, D)
    ret_p = st("rsp")[None]
    ret_s = st("rss").reshape(NCORES * NS, H, DK, DV)[None]
    pool_p = st("pbp")[None]
    pool_s = st("pbs").reshape(NCORES * NS, 15, D)[None]
    conv_p = np.ascontiguousarray(st("cbp").transpose(1, 0, 2, 3))
    conv_s = np.ascontiguousarray(st("cbs").reshape(NCORES, 2, NS, 2, 2 * DFF).transpose(1, 0, 2, 3, 4)).reshape(2, NCORES * NS, 2, 2 * DFF)
    mk = np.ascontiguousarray(st("mkp").transpose(1, 0, 2, 3)).reshape(2, NCORES, NMEM, 4, 256)
    mv = np.ascontiguousarray(st("mvp").transpose(1, 0, 2, 3)).reshape(2, NCORES, NMEM, 4, 256)
    return (y_prompt, y_sample, ret_p, ret_s, pool_p, pool_s, conv_p, conv_s, mk, mv)
```

```python
import math
import numpy as np
import concourse.bass as bass
import concourse.mybir as mybir
from concourse.bass_utils import run_bass_kernel_spmd

F32 = mybir.dt.float32
BF16 = mybir.dt.bfloat16
AF = mybir.ActivationFunctionType
ALU = mybir.AluOpType

NCORES = 8
D = 1024
KC = 8
SEQ = 2048
NS = 16
LS = 8
NT = SEQ + NS * LS
TILES = [(0, 512), (512, 512), (1024, 512), (1536, 512), (2048, 128)]
H = 4
DK = 256
DV = 512
NMEM = 256
DFF = 2816
NFC = DFF // 128
PAST = 16384
EPS = 1e-6
GN_EPS = 1e-5
WSLOT = 4096
NSLOT = 6
SLOTB = 512

V_MIX, V_XA, V_FFN, V_FIN, V_MEM, V_PSC = 0, 16, 32, 48, 56, 72
V_CW = 80
V_CB = V_CW + 6 * 44
NVEC = V_CB + 2 * 44


class Sch:
    LIMIT = 12000
    R = 6

    def __init__(self, nc):
        self.nc = nc
        self.E = {'pe': nc.tensor, 'act': nc.scalar, 'dve': nc.vector, 'pool': nc.gpsimd, 'sp': nc.sync}
        self.sem = {}
        self.cnt = {}
        self.nsem = 0
        for e in ('pe', 'act', 'dve', 'pool'):
            self.sem[e] = self._newsem(e)
            self.cnt[e] = 0
        self.dq = {}
        for q in ('sp', 'pool'):
            self.dq[q] = {'sems': [self._newsem('d' + q) for _ in range(self.R)], 'vals': [0] * self.R, 'i': 0}
        self.seen = {e: {} for e in self.E}
        self.lastw = {}
        self.reads = {}
        self.nwait = 0
        self.nins = 0
        self.ps_state = {}
        self.know = {}

    def _newsem(self, tag):
        self.nsem += 1
        return self.nc.alloc_semaphore('%s_%d' % (tag, self.nsem))

    @staticmethod
    def _exp(keys):
        out = []
        for k in keys:
            if isinstance(k, list):
                out.extend(k)
            else:
                out.append(k)
        return out

    def _need(self, src, reads, writes):
        need = {}

        def add(ev):
            sem, val, _ = ev
            if need.get(sem, 0) < val:
                need[sem] = val
        for r in reads:
            for ev in self.lastw.get(r, ()):
                add(ev)
        same_ok = (src == 'pe')
        for w in writes:
            for ev in self.lastw.get(w, ()):
                if ev[2] != src or not same_ok:
                    add(ev)
            rd = self.reads.get(w)
            if rd:
                for ev in rd.values():
                    if ev[2] != src or not same_ok:
                        add(ev)
        return need

    def _waits(self, eng, need, attach=False):
        seen = self.seen[eng]
        todo = []
        for sem, val in sorted(need.items(), key=lambda kv: -len(self.know.get((kv[0], kv[1]), ()))):
            if seen.get(sem, 0) >= val:
                continue
            todo.append((sem, val))
            seen[sem] = val
            for s2, v2 in self.know.get((sem, val), {}).items():
                if seen.get(s2, 0) < v2:
                    seen[s2] = v2
        last = None
        if attach and todo:
            last = todo.pop()
        for sem, val in todo:
            self.E[eng].wait_ge(sem, val)
            self.nwait += 1
        return last

    def _record(self, ev, reads, writes):
        sem, val, src = ev
        q = src[4:] if src.startswith('dma_') else src
        self.know[(sem, val)] = dict(self.seen[q])
        for w in writes:
            if isinstance(w, tuple) and w[0] == 'ps':
                self.ps_state[w] = 'w' if src == 'pe' else 'r'
        for r in reads:
            self.reads.setdefault(r, {})[(src, sem)] = ev
        for w in writes:
            if src.startswith('dma') and not self.reads.get(w):
                self.lastw[w] = [p for p in self.lastw.get(w, ()) if p[2] == src] + [ev]
            else:
                self.lastw[w] = [ev]
            self.reads[w] = {}

    def _bump(self, eng, ins):
        self.cnt[eng] += 1
        ins.then_inc(self.sem[eng], 1)
        ev = (self.sem[eng], self.cnt[eng], eng)
        if self.cnt[eng] >= self.LIMIT:
            self.sem[eng] = self._newsem(eng)
            self.cnt[eng] = 0
        return ev

    def _pexcl(self, reads, writes):
        ex = [r for r in reads if isinstance(r, tuple) and r[0] == 'ps' and r not in writes]
        return writes + ex

    def op(self, eng, fn, reads=(), writes=()):
        reads = self._exp(reads)
        writes = self._pexcl(reads, self._exp(writes))
        last = self._waits(eng, self._need(eng, reads, writes), attach=True)
        ins = fn(self.E[eng])
        if last is not None:
            ins._wait_ge(last[0], last[1])
        self.nins += 1
        self._record(self._bump(eng, ins), reads, writes)

    def group(self, eng, fns, reads=(), writes=()):
        reads = self._exp(reads)
        writes = self._pexcl(reads, self._exp(writes))
        last = self._waits(eng, self._need(eng, reads, writes), attach=True)
        ins = None
        for fn in fns:
            ins = fn(self.E[eng])
            if last is not None:
                ins._wait_ge(last[0], last[1])
                last = None
            self.nins += 1
        self._record(self._bump(eng, ins), reads, writes)

    def dma(self, q, out, in_, reads=(), writes=()):
        reads = self._exp(reads)
        writes = self._exp(writes)
        src = 'dma_' + q
        need = self._need(src, reads, writes)
        d = self.dq[q]
        j = d['i'] % self.R
        d['i'] += 1
        sem = d['sems'][j]
        if d['vals'][j] > 0 and need.get(sem, 0) < d['vals'][j]:
            need[sem] = d['vals'][j]
        self._waits(q, need)
        ins = self.E[q].dma_start(out=out, in_=in_)
        ins.then_inc(sem, 16)
        self.nins += 1
        d['vals'][j] += 16
        self._record((sem, d['vals'][j], src), reads, writes)

    def finish(self):
        need = {}
        for q in self.dq.values():
            for s, v in zip(q['sems'], q['vals']):
                if v > 0:
                    need[s] = v
        for e in ('pe', 'act', 'dve', 'pool'):
            if self.cnt[e] > 0:
                need[self.sem[e]] = self.cnt[e]
        self._waits('sp', need)


class Arena:
    def __init__(self, nc, nbytes):
        self.nc = nc
        self.h = nc.alloc_sbuf_tensor('arena', [128, nbytes // 4], F32)
        self.base = nc.lookup_mloc(self.h).addr
        self.size = nbytes
        self.off = 0
        self.n = 0

    def reset(self, off=0):
        self.off = off

    def mark(self):
        return self.off

    def alloc(self, name, shape, dtype):
        esz = 4 if dtype == F32 else 2
        nb = esz
        for s in shape[1:]:
            nb *= s
        nb = (nb + 31) // 32 * 32
        self.off = (self.off + SLOTB - 1) // SLOTB * SLOTB
        assert self.off + nb <= self.size, (name, self.off, nb, self.size)
        self.n += 1
        t = self.nc.alloc_sbuf_tensor_at('%s_%d' % (name, self.n), list(shape), dtype, offset=self.base + self.off)
        keys = [('ar', s) for s in range(self.off // SLOTB, (self.off + nb - 1) // SLOTB + 1)]
        self.off += nb
        return t, keys


def build_program():
    nc = bass.Bass("TRN2", target_bir_lowering=False)

    def din(name, shape):
        return nc.dram_tensor(name, list(shape), F32, kind="ExternalInput").ap()

    def dout(name, shape):
        return nc.dram_tensor(name, list(shape), F32, kind="ExternalOutput").ap()

    x_d = din("x", [SEQ, D])
    xs_d = din("xs", [NS * LS, D])
    mem_d = din("mem", [NMEM, D])
    cmk_d = din("cmk", [2, NS, NMEM, D])
    cmv_d = din("cmv", [2, NS, NMEM, D])
    sret_d = din("sret", [NS, H, DK, DV])
    cpool_d = din("cpool", [NS * 15, D])
    cconv_d = din("cconv", [2, NS * 2, 2 * DFF])
    w_in_d = din("w_in", [D, 6144])
    gn_d = din("gn", [1, 2048])
    w_out_d = din("w_out", [2048, D])
    pw_d = din("pw", [4, 256, 256])
    wq_d = din("wq", [2, D, D])
    wk_d = din("wk", [2, D, D])
    wv_d = din("wv", [2, D, D])
    wo_d = din("wo", [2, D, D])
    wup_d = din("wup", [2, D, 2 * DFF])
    wdn_d = din("wdn", [2, DFF, D])
    vec_d = din("vec", [128, NVEC])
    cst_d = din("cst", [128, 128 + 16 + 2 + 248 + 256])
    cos_d = din("cosT", [128, NT])
    sin_d = din("sinT", [128, NT])
    hcf_d = din("hcf", [H, 128, 258])
    hcq_d = din("hcq", [H, 128, 640])
    icn_d = din("icn", [128, 4 * 512])
    pmat_d = din("pmat", [128, 24 * 128])

    y_d = dout("y", [SEQ, D])
    ys_d = dout("ys", [NS * LS, D])
    rsp_d = dout("rsp", [H, DK, DV])
    rss_d = dout("rss", [NS, H, DK, DV])
    pbp_d = dout("pbp", [15, D])
    pbs_d = dout("pbs", [NS, 15, D])
    cbp_d = dout("cbp", [2, 2, 2 * DFF])
    cbs_d = dout("cbs", [2, NS * 2, 2 * DFF])
    mkp_d = dout("mkp", [2, NMEM, D])
    mvp_d = dout("mvp", [2, NMEM, D])

    S = Sch(nc)

    xT = nc.alloc_sbuf_tensor("sb_xT", [128, KC, NT], F32)
    hT = nc.alloc_sbuf_tensor("sb_hT", [128, KC, NT], BF16)
    wsl = [nc.alloc_sbuf_tensor("wsl%d" % i, [128, WSLOT], BF16) for i in range(NSLOT)]
    vec = nc.alloc_sbuf_tensor("sb_vec", [128, NVEC], F32)
    cst = nc.alloc_sbuf_tensor("sb_cst", [128, 128 + 16 + 2], F32)
    cstb = nc.alloc_sbuf_tensor("sb_cstb", [128, 128 + 248 + 256], BF16)
    ident = cst[:, 0:128]
    rowmask = cst[:, 128:144]
    epsc = cst[:, 144:145]
    gepsc = cst[:, 145:146]
    identb = cstb[:, 0:128]
    colmask = cstb[:, 128:376]
    onesD = cstb[:, 376:504]
    ones1 = cstb[:, 504:632]
    remaining = nc.sbuf_bytes_remaining
    AR = Arena(nc, (remaining - 64) // SLOTB * SLOTB)

    psb = [nc.alloc_psum_tensor("ps%d" % i, [128, 512], F32) for i in range(7)]
    psbf = nc.alloc_psum_tensor("psbf", [128, 1024], BF16)
    PSBK = ('ps', 7)
    ps_i = [0]
    NROT = [6]

    def ps_set(n):
        NROT[0] = n

    def ps_next():
        i = ps_i[0] % NROT[0]
        ps_i[0] += 1
        assert S.ps_state.get(('ps', i)) != 'w', ("PSUM bank re-allocated before its consumer was emitted", i)
        return psb[i], ('ps', i)
    PSL = [(psb[4], ('ps', 4)), (psb[5], ('ps', 5))]
    PST, PSTK = psb[6], ('ps', 6)

    def xk(t):
        return ('xT', t)

    def hk(t):
        return ('hT', t)

    def mm(out, lhsT, rhs, start, stop):
        return lambda e: e.matmul(out, lhsT, rhs, start=start, stop=stop)

    def tr(out, in_, idn):
        return lambda e: e.transpose(out, in_, idn)

    wnext = [0]

    def wload(src_ap_list):
        i = wnext[0] % NSLOT
        wnext[0] += 1
        key = ('W', i)
        for (c0, nk, ncol, ap) in src_ap_list:
            dst = wsl[i][:, c0:c0 + nk * ncol].rearrange("p (k n) -> p k n", k=nk)
            S.dma('pool', dst, ap, writes=[key])
        return wsl[i], key

    def wview(slot, nk, ncol, c0=0):
        return slot[:, c0:c0 + nk * ncol].rearrange("p (k n) -> p k n", k=nk)

    def wload2(nk, ncol, parts):
        i = wnext[0] % NSLOT
        wnext[0] += 1
        key = ('W', i)
        v = wview(wsl[i], nk, ncol)
        for (sel, ap) in parts:
            S.dma('pool', sel(v), ap, writes=[key])
        return v, key

    def kpn(ap):
        return ap.rearrange("(k p) n -> p k n", p=128)

    ALL = lambda v: v

    S.dma('sp', vec[:], vec_d, writes=['vec'])
    S.dma('sp', cst[:], cst_d[:, 0:146], writes=['cst'])
    S.dma('pool', cstb[:, 0:128], cst_d[:, 0:128], writes=['cstb'])
    S.dma('pool', cstb[:, 128:632], cst_d[:, 146:650], writes=['cstb'])

    def norm_stats(src, srckeys, c0, n, tmp):
        i = tmp['i']
        tmp['i'] += 1
        sq, sqk = tmp['sq'][i % len(tmp['sq'])]
        ms, msk = tmp['ms'][i % len(tmp['ms'])]
        rs, rsk = tmp['rs'][i % len(tmp['rs'])]
        pst, pk = ps_next()
        S.op('act', lambda e: e.activation(sq[:, :, 0:n], src[:, :, c0:c0 + n], AF.Square), reads=srckeys, writes=sqk)
        S.group('pe', [mm(pst[:, 0:n], onesD, sq[:, kc, 0:n], kc == 0, kc == KC - 1) for kc in range(KC)],
                reads=sqk + ['cstb'], writes=[pk])
        S.op('act', lambda e: e.activation(ms[:, 0:n], pst[:, 0:n], AF.Ln, bias=epsc[:, 0:1]), reads=[pk, 'cst'], writes=msk)
        S.op('act', lambda e: e.activation(rs[:, 0:n], ms[:, 0:n], AF.Exp, scale=-0.5), reads=msk, writes=rsk)
        return rs, rsk

    def norm_cols(src, srckeys, c0, n, gcol, dst_fn, dstkeys, tmp):
        rs, rsk = norm_stats(src, srckeys, c0, n, tmp)
        for kc in range(KC):
            S.op('dve', lambda e, kc=kc: e.scalar_tensor_tensor(
                dst_fn(kc), src[:, kc, c0:c0 + n], vec[:, gcol + kc:gcol + kc + 1], rs[:, 0:n], ALU.mult, ALU.mult),
                reads=srckeys + [rsk, 'vec'], writes=dstkeys)

    def norm_tmp(w=512, nb=2):
        return {'sq': [AR.alloc('sq', [128, KC, w], BF16) for _ in range(nb)],
                'ms': [AR.alloc('ms', [128, w], F32) for _ in range(nb)],
                'rs': [AR.alloc('rstd', [128, w], F32) for _ in range(nb)], 'i': 0, 'w': w}

    def norm_all(gcol):
        AR.reset()
        tmp = norm_tmp()
        for t, (t0, n) in enumerate(TILES):
            norm_cols(xT, [xk(t)], t0, n, gcol, lambda kc, t0=t0, n=n: hT[:, kc, t0:t0 + n], [hk(t)], tmp)

    def hoist_norm(gcol, t, tmp):
        t0, n = TILES[t]
        w = tmp['w']
        for c0 in range(t0, t0 + n, w):
            m = min(w, t0 + n - c0)
            norm_cols(xT, [xk(t)], c0, m, gcol, lambda kc, c0=c0, m=m: hT[:, kc, c0:c0 + m], [hk(t)], tmp)

    def io_in(nn=None):
        AR.reset()
        xin = [AR.alloc('xin', [128, D], F32) for _ in range(3)]
        ntmp = norm_tmp(512, 1) if nn is not None else None
        for i in range(17):
            xi, xik = xin[i % 3]
            src = x_d[i * 128:(i + 1) * 128, :] if i < 16 else xs_d
            S.dma('sp', xi[:], src, writes=[xik])
            for half in range(2):
                ptt, ptk = ps_next()
                S.group('pe', [tr(ptt[:, j * 128:(j + 1) * 128], xi[:, (half * 4 + j) * 128:(half * 4 + j + 1) * 128], ident)
                               for j in range(4)], reads=[xik, 'cst'], writes=[ptk])
                S.op('act', lambda e, half=half, i=i, ptt=ptt: e.activation(
                    xT[:, half * 4:half * 4 + 4, i * 128:(i + 1) * 128],
                    ptt[:].rearrange("p (j n) -> p j n", j=4), AF.Copy),
                    reads=[ptk], writes=[xk(i // 4)])
            if nn is not None and (i % 4 == 3 or i == 16):
                hoist_norm(nn, i // 4, ntmp)

    def memkv(li):
        AR.reset()
        KT = AR.alloc('KT', [128, KC, NMEM], BF16)
        Vb = AR.alloc('Vb', [128, 2, D], BF16)
        keep = AR.mark()
        tmp = norm_tmp(NMEM, 1)
        memin = AR.alloc('memin', [128, 2, D], F32)
        memT = AR.alloc('memT', [128, KC, NMEM], F32)
        hmT = AR.alloc('hmT', [128, KC, NMEM], BF16)
        kst = [AR.alloc('kst', [128, D], F32) for _ in range(2)]
        wks = [wload([(0, KC, 512, wk_d[li][:, hf * 512:(hf + 1) * 512].rearrange("(k p) n -> p k n", p=128))]) for hf in range(2)]
        wvs = [wload([(0, KC, 512, wv_d[li][:, hf * 512:(hf + 1) * 512].rearrange("(k p) n -> p k n", p=128))]) for hf in range(2)]
        S.dma('sp', memin[0][:], mem_d.rearrange("(m p) f -> p m f", p=128), writes=memin[1])
        for mc in range(2):
            for half in range(2):
                S.group('pe', [tr(PST[:, j * 128:(j + 1) * 128], memin[0][:, mc, (half * 4 + j) * 128:(half * 4 + j + 1) * 128], ident)
                               for j in range(4)], reads=memin[1] + ['cst'], writes=[PSTK])
                S.op('act', lambda e, half=half, mc=mc: e.activation(
                    memT[0][:, half * 4:half * 4 + 4, mc * 128:(mc + 1) * 128],
                    PST[:].rearrange("p (j n) -> p j n", j=4), AF.Copy), reads=[PSTK], writes=memT[1])
        norm_cols(memT[0], memT[1], 0, NMEM, V_MEM + li * 8, lambda kc: hmT[0][:, kc, :], hmT[1], tmp)
        ki = 0
        for (wsx, out_d, isv) in ((wks, mkp_d, False), (wvs, mvp_d, True)):
            for mc in range(2):
                ks, ksk = kst[ki % 2]
                ki += 1
                for hf in range(2):
                    w, wkey = wsx[hf]
                    wv_ = wview(w, KC, 512)
                    pt, pk = ps_next()
                    S.group('pe', [mm(pt[:, :], hmT[0][:, kc, mc * 128:(mc + 1) * 128], wv_[:, kc, :], kc == 0, kc == KC - 1)
                                   for kc in range(KC)], reads=hmT[1] + [wkey], writes=[pk])
                    S.op('act', lambda e, ks=ks, hf=hf, pt=pt: e.activation(ks[:, hf * 512:(hf + 1) * 512], pt[:, :], AF.Copy),
                         reads=[pk], writes=ksk)
                    if isv:
                        S.op('dve', lambda e, hf=hf, ks=ks, mc=mc: e.tensor_copy(Vb[0][:, mc, hf * 512:(hf + 1) * 512], ks[:, hf * 512:(hf + 1) * 512]),
                             reads=ksk, writes=Vb[1])
                S.dma('sp', out_d[li][mc * 128:(mc + 1) * 128, :], ks[:], reads=ksk)
        for cc in range(KC):
            w, wkey = wks[cc // 4]
            wv_ = wview(w, KC, 512)
            pt, pk = ps_next()
            S.group('pe', [mm(pt[:, 0:NMEM], wv_[:, kc, (cc % 4) * 128:(cc % 4 + 1) * 128], hmT[0][:, kc, :], kc == 0, kc == KC - 1)
                           for kc in range(KC)], reads=hmT[1] + [wkey], writes=[pk])
            S.op('act', lambda e, cc=cc, pt=pt: e.activation(KT[0][:, cc, :], pt[:, 0:NMEM], AF.Copy), reads=[pk], writes=KT[1])
        return KT, Vb, keep

    def resid_add(pt, pk, dc, t, t0, n, scol=None):
        if scol is None:
            S.op('dve', lambda e: e.tensor_tensor(xT[:, dc, t0:t0 + n], pt[:, 0:n], xT[:, dc, t0:t0 + n], ALU.add),
                 reads=[pk, xk(t)], writes=[xk(t)])
        else:
            S.op('dve', lambda e: e.scalar_tensor_tensor(xT[:, dc, t0:t0 + n], pt[:, 0:n], scol, xT[:, dc, t0:t0 + n],
                                                        ALU.mult, ALU.add),
                 reads=[pk, xk(t), 'vec'], writes=[xk(t)])

    def retention_head(h, nn=None):
        lg = math.log(1.0 - 2.0 ** (-5.0 - h))
        cdecP = math.exp(128 * lg)
        cdecS = math.exp(LS * lg)
        AR.reset()
        wqk, wqkk = wload2(KC, 512, [(lambda v: v[:, :, 0:256], kpn(w_in_d[:, h * 256:(h + 1) * 256])),
                                     (lambda v: v[:, :, 256:512], kpn(w_in_d[:, 1024 + h * 256:1024 + (h + 1) * 256]))])
        wv, wvk = wload2(KC, 512, [(ALL, kpn(w_in_d[:, 2048 + h * 512:2048 + (h + 1) * 512]))])
        wg, wgk = wload2(KC, 512, [(ALL, kpn(w_in_d[:, 4096 + h * 512:4096 + (h + 1) * 512]))])
        wo, wok = wload2(4, 1024, [(ALL, kpn(w_out_d[h * 512:(h + 1) * 512, :]))])
        hcf, hcfk = AR.alloc('hcf', [128, 258], F32)
        hcq, hcqk = AR.alloc('hcq', [128, 640], BF16)
        gng, gngk = AR.alloc('gng', [128, 512], F32)
        S.dma('sp', hcf[:], hcf_d[h], writes=hcfk)
        S.dma('pool', hcq[:], hcq_d[h], writes=hcqk)
        S.dma('sp', gng[:], gn_d[0:1, h * 512:(h + 1) * 512].partition_broadcast(128), writes=gngk)
        AR.alloc('al', [128, 8], F32)
        o_cs = AR.mark()
        cos, cosk = AR.alloc('cos', [128, 512], F32)
        sin, sink = AR.alloc('sin', [128, 512], F32)
        ta, tak = AR.alloc('ta', [128, 512], F32)
        tb, tbk = AR.alloc('tb', [128, 512], F32)
        qT, qTk = AR.alloc('qT', [128, 2, 512], BF16)
        kT, kTk = AR.alloc('kT', [128, 2, 512], BF16)
        qdT, qdTk = AR.alloc('qdT', [128, 2, 512], BF16)
        vS, vSk = AR.alloc('vS', [128, 4, 512], BF16)
        sg, sgk = AR.alloc('sg', [128, 4, 512], BF16)
        ATs = [AR.alloc('AT', [128, 128], BF16) for _ in range(2)]
        kds = [AR.alloc('kd', [128, 256], BF16) for _ in range(2)]
        sts = [AR.alloc('st', [128, 16], F32) for _ in range(2)]
        ons = [AR.alloc('on', [128, 512], BF16) for _ in range(2)]
        ogs = [AR.alloc('og', [128, 512], BF16) for _ in range(2)]
        ogT, ogTk = AR.alloc('ogT', [128, 4, 512], BF16)
        mk_ = AR.mark()
        Sf, Sfk = AR.alloc('Sf', [128, 2, 512], F32)
        Sb, Sbk = AR.alloc('Sb', [128, 2, 512], BF16)
        AR.reset(mk_)
        S0bs = [AR.alloc('S0b', [128, 2, 512], BF16) for _ in range(2)]
        S0fs = [AR.alloc('S0f', [128, 2, 512], F32) for _ in range(2)]
        sv_ = AR.mark()
        AR.reset(o_cs)
        S0fs += [AR.alloc('S0f', [128, 2, 512], F32) for _ in range(2)]
        ntmp = None
        if nn is not None:
            AR.reset(o_cs)
            ntmp = norm_tmp(256, 1)
        AR.reset(sv_)
        Zbs = [AR.alloc('Zb', [128, 2, 128], BF16) for _ in range(2)]
        kdzs = [AR.alloc('kdz', [128, 256], BF16) for _ in range(2)]
        cnt = [0]

        CH = {}

        def projA(t):
            t0, n = TILES[t]
            sample = (t == 4)
            nsub = n // 128
            S.dma('sp', cos[:, 0:n], cos_d[:, t0:t0 + n], writes=cosk)
            S.dma('sp', sin[:, 0:n], sin_d[:, t0:t0 + n], writes=sink)
            pq = []
            for cc in range(4):
                pt, pk = ps_next()
                S.group('pe', [mm(pt[:, 0:n], wqk[:, kc, cc * 128:(cc + 1) * 128], hT[:, kc, t0:t0 + n], kc == 0, kc == KC - 1)
                               for kc in range(KC)], reads=[hk(t), wqkk], writes=[pk])
                pq.append((pt, pk))
            for which, (dst, dkey) in enumerate(((qT, qTk), (kT, kTk))):
                (pa, pak), (pb, pbk) = pq[2 * which], pq[2 * which + 1]
                S.op('dve', lambda e: e.tensor_tensor(ta[:, 0:n], pa[:, 0:n], cos[:, 0:n], ALU.mult), reads=[pak, cosk], writes=tak)
                S.op('dve', lambda e: e.tensor_tensor(tb[:, 0:n], pb[:, 0:n], sin[:, 0:n], ALU.mult), reads=[pbk, sink], writes=tbk)
                S.op('dve', lambda e: e.tensor_tensor(dst[:, 0, 0:n], ta[:, 0:n], tb[:, 0:n], ALU.subtract), reads=[tak, tbk], writes=dkey)
                S.op('dve', lambda e: e.tensor_tensor(ta[:, 0:n], pb[:, 0:n], cos[:, 0:n], ALU.mult), reads=[pbk, cosk], writes=tak)
                S.op('dve', lambda e: e.tensor_tensor(tb[:, 0:n], pa[:, 0:n], sin[:, 0:n], ALU.mult), reads=[pak, sink], writes=tbk)
                S.op('dve', lambda e: e.tensor_tensor(dst[:, 1, 0:n], ta[:, 0:n], tb[:, 0:n], ALU.add), reads=[tak, tbk], writes=dkey)
            qo = 512 if sample else 0
            for c in range(2):
                S.op('dve', lambda e, c=c: e.tensor_tensor(qdT[:, c, 0:n], qT[:, c, 0:n], hcq[:, qo:qo + n], ALU.mult),
                     reads=[qTk, hcqk], writes=qdTk)
            for s_ in range(nsub):
                c0 = t0 + s_ * 128
                pt, pk = ps_next()
                S.group('pe', [mm(pt[:, :], hT[:, kc, c0:c0 + 128], wv[:, kc, :], kc == 0, kc == KC - 1) for kc in range(KC)],
                        reads=[hk(t), wvk], writes=[pk])
                S.op('act', lambda e, s_=s_, pt=pt: e.activation(vS[:, s_, :], pt[:, :], AF.Copy), reads=[pk], writes=vSk)

        def projA2(t):
            t0, n = TILES[t]
            nsub = n // 128
            for s_ in range(nsub):
                c0 = t0 + s_ * 128
                pt, pk = ps_next()
                S.group('pe', [mm(pt[:, :], hT[:, kc, c0:c0 + 128], wg[:, kc, :], kc == 0, kc == KC - 1) for kc in range(KC)],
                        reads=[hk(t), wgk], writes=[pk])
                S.op('act', lambda e, s_=s_, pt=pt: e.activation(sg[:, s_, :], pt[:, :], AF.Silu), reads=[pk], writes=sgk)
                S.op('dve', lambda e, s_=s_: e.tensor_tensor(sg[:, s_, :], sg[:, s_, :], gng[:, :], ALU.mult), reads=sgk + gngk, writes=sgk)

        def st1(t, s_):
            sample = (t == 4)
            cs = slice(s_ * 128, (s_ + 1) * 128)
            mask = hcf[:, 128:256] if sample else hcf[:, 0:128]
            kdc = hcf[:, 257:258] if sample else hcf[:, 256:257]
            i2 = cnt[0] % 2
            cnt[0] += 1
            AT, ATk = ATs[i2]
            kd, kdk = kds[i2]
            pt, pk = ps_next()
            S.group('pe', [mm(pt[:, 0:128], kT[:, c, cs], qT[:, c, cs], c == 0, c == 1) for c in range(2)],
                    reads=[kTk, qTk], writes=[pk])
            S.op('dve', lambda e: e.tensor_tensor(AT[:, :], pt[:, 0:128], mask, ALU.mult), reads=[pk, hcfk], writes=ATk)
            S.group('pe', [tr(psbf[:, c * 128:(c + 1) * 128], kT[:, c, cs], identb) for c in range(2)],
                    reads=[kTk, 'cstb'], writes=[PSBK])
            S.op('act', lambda e: e.activation(kd[:, :], psbf[:, 0:256], AF.Identity, scale=kdc), reads=[PSBK, hcfk], writes=kdk)
            CH[(t, s_)] = dict(i2=i2, AT=AT, ATk=ATk, kd=kd, kdk=kdk)

        def st2(t, s_):
            sample = (t == 4)
            chunk = t * 4 + s_
            cs = slice(s_ * 128, (s_ + 1) * 128)
            c_ = CH[(t, s_)]
            AT, ATk, kd, kdk = c_['AT'], c_['ATk'], c_['kd'], c_['kdk']
            if not sample:
                po, pok = PSL[s_ % 2]
                for c in range(2):
                    pn, pnk = ps_next()
                    S.op('pe', mm(pn[:, :], kd[:, c * 128:(c + 1) * 128], vS[:, s_, :], True, True), reads=[kdk, vSk], writes=[pnk])
                    c_.setdefault('pn', []).append((pn, pnk))
            else:
                po, pok = PSL[0]
                S.op('pe', mm(po[:, :], AT[:, :], vS[:, 0, :], True, False), reads=[ATk, vSk], writes=[pok])
                sv = lambda b: sret_d[b, h].rearrange("(c p) v -> p c v", p=128)

                def load(b):
                    S.dma('sp', S0fs[b % 4][0][:], sv(b), writes=S0fs[b % 4][1])

                def prep(b):
                    S0f, S0fk = S0fs[b % 4]
                    S0b, S0bk = S0bs[b % 2]
                    Zb, Zbk = Zbs[b % 2]
                    kdz, kdzk = kdzs[b % 2]
                    S.op('act', lambda e: e.activation(S0b[:].rearrange("p c v -> p (c v)"), S0f[:].rearrange("p c v -> p (c v)"), AF.Copy),
                         reads=S0fk, writes=S0bk)
                    for c in range(2):
                        S.op('dve', lambda e, c=c: e.tensor_tensor(Zb[:, c, :], qdT[:, c, 0:128], colmask[:, 120 - 8 * b:248 - 8 * b], ALU.mult),
                             reads=[qdTk, 'cstb'], writes=Zbk)
                    S.op('dve', lambda e: e.tensor_scalar(kdz[:, :], kd[:, :], rowmask[:, b:b + 1], None, ALU.mult),
                         reads=[kdk, 'cst'], writes=kdzk)
                for b in range(3):
                    load(b)
                prep(0)
                for b in range(NS):
                    S0f, S0fk = S0fs[b % 4]
                    S0b, S0bk = S0bs[b % 2]
                    Zb, Zbk = Zbs[b % 2]
                    kdz, kdzk = kdzs[b % 2]
                    if b + 3 < NS:
                        load(b + 3)
                    pns = []
                    for c in range(2):
                        pn, pnk = ps_next()
                        S.op('pe', mm(pn[:, :], kdz[:, c * 128:(c + 1) * 128], vS[:, 0, :], True, True), reads=[kdzk, vSk], writes=[pnk])
                        pns.append((pn, pnk))
                    for c in range(2):
                        S.op('pe', mm(po[:, :], Zb[:, c, :], S0b[:, c, :], False, (b == NS - 1 and c == 1)),
                             reads=[Zbk, S0bk], writes=[pok])
                    if b + 1 < NS:
                        prep(b + 1)
                    for c in range(2):
                        pn, pnk = pns[c]
                        S.op('dve', lambda e, c=c, pn=pn: e.scalar_tensor_tensor(S0f[:, c, :], S0f[:, c, :], cdecS, pn[:, :], ALU.mult, ALU.add),
                             reads=[pnk, S0fk], writes=S0fk)
                    S.dma('pool', rss_d[b, h].rearrange("(c p) v -> p c v", p=128), S0f[:], reads=S0fk)
            c_['po'], c_['pok'] = po, pok

        def st2o(t, s_):
            if t == 4:
                return
            chunk = t * 4 + s_
            cs = slice(s_ * 128, (s_ + 1) * 128)
            c_ = CH[(t, s_)]
            AT, ATk = c_['AT'], c_['ATk']
            po, pok = c_['po'], c_['pok']
            fns = [mm(po[:, :], AT[:, :], vS[:, s_, :], True, chunk == 0)]
            rd = [ATk, vSk]
            if chunk > 0:
                fns += [mm(po[:, :], qdT[:, c, cs], Sb[:, c, :], False, c == 1) for c in range(2)]
                rd += [qdTk, Sbk]
            S.group('pe', fns, reads=rd, writes=[pok])
            for c in range(2):
                pn, pnk = c_['pn'][c]
                if chunk == 0:
                    S.op('act', lambda e, c=c, pn=pn: e.activation(Sf[:, c, :], pn[:, :], AF.Copy), reads=[pnk], writes=Sfk)
                else:
                    S.op('dve', lambda e, c=c, pn=pn: e.scalar_tensor_tensor(Sf[:, c, :], Sf[:, c, :], cdecP, pn[:, :], ALU.mult, ALU.add),
                         reads=[pnk, Sfk], writes=Sfk)
            if chunk == 15:
                S.dma('sp', rsp_d[h].rearrange("(c p) v -> p c v", p=128), Sf[:], reads=Sfk)
            else:
                S.op('act', lambda e: e.activation(Sb[:].rearrange("p c v -> p (c v)"), Sf[:].rearrange("p c v -> p (c v)"), AF.Copy),
                     reads=Sfk, writes=Sbk)

        def c1(t, s_):
            c_ = CH[(t, s_)]
            po, pok, i2 = c_['po'], c_['pok'], c_['i2']
            st, stk = sts[i2]
            S.op('dve', lambda e: e.bn_stats(st[:, 0:6], po[:, :]), reads=[pok], writes=stk)
            S.op('dve', lambda e: e.bn_aggr(st[:, 6:8], st[:, 0:6]), reads=stk, writes=stk)
            S.op('act', lambda e: e.activation(st[:, 8:9], st[:, 7:8], AF.Sqrt, bias=gepsc[:, 0:1]), reads=stk + ['cst'], writes=stk)

        def c2(t, s_):
            c_ = CH[(t, s_)]
            po, pok, i2 = c_['po'], c_['pok'], c_['i2']
            st, stk = sts[i2]
            on, onk = ons[i2]
            S.op('dve', lambda e: e.reciprocal(st[:, 8:9], st[:, 8:9]), reads=stk, writes=stk)
            S.op('dve', lambda e: e.scalar_tensor_tensor(st[:, 9:10], st[:, 6:7], -1.0, st[:, 8:9], ALU.mult, ALU.mult), reads=stk, writes=stk)
            S.op('act', lambda e: e.activation(on[:, :], po[:, :], AF.Identity, bias=st[:, 9:10], scale=st[:, 8:9]),
                 reads=[pok] + stk, writes=onk)

        def c3(t, s_):
            cs = slice(s_ * 128, (s_ + 1) * 128)
            c_ = CH[(t, s_)]
            i2 = c_['i2']
            on, onk = ons[i2]
            og, ogk = ogs[i2]
            S.op('dve', lambda e: e.tensor_tensor(og[:, :], on[:, :], sg[:, s_, :], ALU.mult), reads=onk + sgk, writes=ogk)
            S.group('pe', [tr(psbf[:, 512 + j * 128:512 + (j + 1) * 128], og[:, j * 128:(j + 1) * 128], identb) for j in range(4)],
                    reads=ogk + ['cstb'], writes=[PSBK])
            S.op('act', lambda e: e.activation(ogT[:, :, cs], psbf[:, 512:1024].rearrange("p (j n) -> p j n", j=4), AF.Copy),
                 reads=[PSBK], writes=ogTk)

        def projC(t):
            t0, n = TILES[t]
            for dc in range(KC):
                pt, pk = ps_next()
                S.group('pe', [mm(pt[:, 0:n], wo[:, j, dc * 128:(dc + 1) * 128], ogT[:, j, 0:n], j == 0, j == 3) for j in range(4)],
                        reads=[wok] + ogTk, writes=[pk])
                resid_add(pt, pk, dc, t, t0, n)

        ps_set(4)
        projA(0)
        projA2(0)
        for t, (t0, n) in enumerate(TILES):
            nsub = n // 128
            st1(t, 0)
            for k in range(nsub + 2):
                if k < nsub:
                    st2(t, k)
                if k + 1 < nsub:
                    st1(t, k + 1)
                if k < nsub:
                    st2o(t, k)
                    c1(t, k)
                if 0 <= k - 1 < nsub:
                    c2(t, k - 1)
                if k == nsub and t + 1 < len(TILES):
                    projA(t + 1)
                if 0 <= k - 2 < nsub:
                    c3(t, k - 2)
            if t + 1 < len(TILES):
                projA2(t + 1)
            projC(t)
            if nn is not None:
                hoist_norm(nn, t, ntmp)
        ps_set(6)

    def attention(li, pre_norm=True, nn=None):
        if pre_norm:
            norm_all(V_XA + 8 * li)
        (KT, KTk), (Vb, Vbk), keep = memkv(li)
        AR.reset(keep)
        wq = [wload2(KC, 512, [(ALL, kpn(wq_d[li][:, hf * 512:(hf + 1) * 512]))]) for hf in range(2)]
        wo = [wload2(KC, 512, [(ALL, kpn(wo_d[li][:, hf * 512:(hf + 1) * 512]))]) for hf in range(2)]
        qx, qxk = AR.alloc('qx', [128, KC, 512], BF16)
        oxT, oxTk = AR.alloc('oxT', [128, KC, 512], BF16)
        ETa, ETak = AR.alloc('ETa', [128, 4, 2, 128], BF16)
        rdS, rdSk = AR.alloc('rdS', [128, 4, 128], F32)
        mk_ = AR.mark()
        ETs = [AR.alloc('ET', [128, 2, 512], BF16) for _ in range(2)]
        rdens = [AR.alloc('rden', [128, 512], F32) for _ in range(2)]
        AR.reset(mk_)
        AR.alloc('pad', [128, 2048], F32)
        Kfs = [AR.alloc('Kf', [128, 2, D], F32) for _ in range(2)]
        ntmp = None
        if nn is not None:
            AR.reset(mk_ + 8192)
            ntmp = norm_tmp(512, 1)
        AR.reset(mk_)
        Vss = [AR.alloc('Vs', [128, 2, D], BF16) for _ in range(2)]
        AR.reset(0)
        KTss = [AR.alloc('KTs', [128, KC, NMEM], BF16) for _ in range(2)]

        def qproj(t, t0, n):
            for cc in range(KC):
                w, wk_ = wq[cc // 4]
                pt, pk = ps_next()
                S.group('pe', [mm(pt[:, 0:n], w[:, kc, (cc % 4) * 128:(cc % 4 + 1) * 128], hT[:, kc, t0:t0 + n], kc == 0, kc == KC - 1)
                               for kc in range(KC)], reads=[hk(t), wk_], writes=[pk])
                S.op('act', lambda e, cc=cc, pt=pt: e.activation(qx[:, cc, 0:n], pt[:, 0:n], AF.Copy), reads=[pk], writes=qxk)

        def oproj(t, t0, n):
            for dc in range(KC):
                w, wk_ = wo[dc // 4]
                pt, pk = ps_next()
                S.group('pe', [mm(pt[:, 0:n], w[:, cc, (dc % 4) * 128:(dc % 4 + 1) * 128], oxT[:, cc, 0:n], cc == 0, cc == KC - 1)
                               for cc in range(KC)], reads=[wk_] + oxTk, writes=[pk])
                resid_add(pt, pk, dc, t, t0, n)

        def scoresH(t, hd):
            t0, n = TILES[t]
            et, etk = ETs[hd % 2]
            for mc in range(2):
                pt, pk = ps_next()
                S.group('pe', [mm(pt[:, 0:n], KT[:, 2 * hd + c, mc * 128:(mc + 1) * 128], qx[:, 2 * hd + c, 0:n], c == 0, c == 1)
                               for c in range(2)], reads=KTk + qxk, writes=[pk])
                S.op('act', lambda e, mc=mc, pt=pt: e.activation(et[:, mc, 0:n], pt[:, 0:n], AF.Exp, scale=1.0 / 16.0),
                     reads=[pk], writes=etk)

        def pvH(t, hd):
            t0, n = TILES[t]
            et, etk = ETs[hd % 2]
            rd, rdk = rdens[hd % 2]
            pd, pdk = ps_next()
            S.group('pe', [mm(pd[:, 0:n], ones1, et[:, mc, 0:n], mc == 0, mc == 1) for mc in range(2)],
                    reads=etk + ['cstb'], writes=[pdk])
            S.op('act', lambda e: e.activation(rd[:, 0:n], pd[:, 0:n], AF.Ln), reads=[pdk], writes=rdk)
            S.op('act', lambda e: e.activation(rd[:, 0:n], rd[:, 0:n], AF.Exp, scale=-1.0), reads=rdk, writes=rdk)
            for c in range(2):
                po, pok = ps_next()
                S.group('pe', [mm(po[:, 0:n], Vb[:, mc, (2 * hd + c) * 128:(2 * hd + c + 1) * 128], et[:, mc, 0:n], mc == 0, mc == 1)
                               for mc in range(2)], reads=Vbk + etk, writes=[pok])
                S.op('dve', lambda e, c=c, po=po: e.tensor_tensor(oxT[:, 2 * hd + c, 0:n], po[:, 0:n], rd[:, 0:n], ALU.mult),
                     reads=[pok] + rdk, writes=oxTk)

        ps_set(6)
        qproj(0, *TILES[0])
        for t, (t0, n) in enumerate(TILES[:4]):
            scoresH(t, 0)
            for hd in range(4):
                if hd + 1 < 4:
                    scoresH(t, hd + 1)
                pvH(t, hd)
            qproj(t + 1, *TILES[t + 1])
            oproj(t, t0, n)
            if nn is not None:
                hoist_norm(nn, t, ntmp)

        t, (t0, n) = 4, TILES[4]
        ps_set(4)
        kview = lambda b: cmk_d[li, b].rearrange("(m p) f -> p m f", p=128)
        vview = lambda b: cmv_d[li, b].rearrange("(m p) f -> p m f", p=128)
        S.dma('sp', Kfs[0][0][:], kview(0), writes=Kfs[0][1])
        for b in range(2):
            S.dma('pool', Vss[b][0][:], vview(b), writes=Vss[b][1])
        for b in range(NS):
            Kf, Kfk = Kfs[b % 2]
            KTs, KTsk = KTss[b % 2]
            if b + 1 < NS:
                S.dma('sp', Kfs[(b + 1) % 2][0][:], kview(b + 1), writes=Kfs[(b + 1) % 2][1])
            for mc in range(2):
                for half in range(2):
                    ptt, ptk = ps_next()
                    S.group('pe', [tr(ptt[:, j * 128:(j + 1) * 128], Kf[:, mc, (half * 4 + j) * 128:(half * 4 + j + 1) * 128], ident)
                                   for j in range(4)], reads=Kfk + ['cst'], writes=[ptk])
                    S.op('act', lambda e, mc=mc, half=half, ptt=ptt: e.activation(
                        KTs[:, half * 4:half * 4 + 4, mc * 128:(mc + 1) * 128], ptt[:].rearrange("p (j n) -> p j n", j=4), AF.Copy),
                        reads=[ptk], writes=KTsk)
            for hd in range(4):
                pl, plk = PSL[hd // 2]
                for mc in range(2):
                    co = ((hd % 2) * 2 + mc) * 128 + 8 * b
                    S.group('pe', [mm(pl[:, co:co + 8], KTs[:, 2 * hd + c, mc * 128:(mc + 1) * 128], qx[:, 2 * hd + c, 8 * b:8 * b + 8], c == 0, c == 1)
                                   for c in range(2)], reads=KTsk + qxk, writes=[plk])
        for k in range(2):
            pl, plk = PSL[k]
            S.op('act', lambda e, k=k, pl=pl: e.activation(ETa[:, 2 * k:2 * k + 2, :, :].rearrange("p h m n -> p (h m n)"), pl[:, :], AF.Exp, scale=1.0 / 16.0),
                 reads=[plk], writes=ETak)
        pd, pdk = ps_next()
        for hd in range(4):
            S.group('pe', [mm(pd[:, hd * 128:(hd + 1) * 128], ones1, ETa[:, hd, mc, :], mc == 0, mc == 1) for mc in range(2)],
                    reads=ETak + ['cstb'], writes=[pdk])
        S.op('act', lambda e: e.activation(rdS[:].rearrange("p h n -> p (h n)"), pd[:, :], AF.Ln), reads=[pdk], writes=rdSk)
        S.op('act', lambda e: e.activation(rdS[:].rearrange("p h n -> p (h n)"), rdS[:].rearrange("p h n -> p (h n)"), AF.Exp, scale=-1.0),
             reads=rdSk, writes=rdSk)
        for b in range(NS):
            Vs, Vsk = Vss[b % 2]
            for cc in range(KC):
                pl, plk = PSL[cc // 4]
                co = (cc % 4) * 128 + 8 * b
                S.group('pe', [mm(pl[:, co:co + 8], Vs[:, mc, cc * 128:(cc + 1) * 128], ETa[:, cc // 2, mc, 8 * b:8 * b + 8], mc == 0, mc == 1)
                               for mc in range(2)], reads=Vsk + ETak, writes=[plk])
            if b + 2 < NS:
                S.dma('pool', Vs[:], vview(b + 2), writes=Vsk)
        for hd in range(4):
            pl, plk = PSL[hd // 2]
            co = ((2 * hd) % 4) * 128
            S.op('dve', lambda e, hd=hd, pl=pl, co=co: e.tensor_tensor(
                oxT[:, 2 * hd:2 * hd + 2, 0:128], pl[:, co:co + 256].rearrange("p (c n) -> p c n", c=2),
                rdS[:, hd, :].unsqueeze(1).to_broadcast([128, 2, 128]), ALU.mult),
                reads=[plk] + rdSk, writes=oxTk)
        oproj(t, t0, n)
        if nn is not None:
            hoist_norm(nn, t, ntmp)
        ps_set(6)

    def ffn(li, pre_norm=True, nn=None):
        if pre_norm:
            norm_all(V_FFN + 8 * li)
        AR.reset()
        convcT, convcTk = AR.alloc('convcT', [128, 2 * NFC, 32], F32)
        haloP, haloPk = AR.alloc('haloP', [128, 2 * NFC, 2], F32)
        hts = [AR.alloc('ht', [128, 2 * NFC, 2], F32) for _ in range(2)]
        htk = [hts[0][1], hts[1][1]]
        ht1, ht1k = AR.alloc('ht1', [128, 2 * NFC], F32)
        cAs = [AR.alloc('cA', [128, 512], F32) for _ in range(2)]
        cGs = [AR.alloc('cG', [128, 512], F32) for _ in range(2)]
        sgs = [AR.alloc('sgt', [128, 512], F32) for _ in range(2)]
        mTs = [AR.alloc('mT', [128, 4, 512], BF16) for _ in range(2)]
        stgs = [AR.alloc('stg', [32, 512], F32) for _ in range(4)]
        ntmp = norm_tmp(512, 1) if nn is not None else None
        cw = lambda j, cc: vec[:, V_CW + (li * 3 + j) * 44 + cc:V_CW + (li * 3 + j) * 44 + cc + 1]
        cb = lambda cc: vec[:, V_CB + li * 44 + cc:V_CB + li * 44 + cc + 1]
        def cache_prep():
            for pc in range(11):
                stg, stgk = stgs[pc % 2]
                S.dma('sp', stg[:], cconv_d[li][:, pc * 512:(pc + 1) * 512], writes=stgk)
                S.group('pe', [tr(PST[:, j * 32:(j + 1) * 32], stg[0:32, j * 128:(j + 1) * 128], ident[0:32, 0:32]) for j in range(4)],
                        reads=stgk + ['cst'], writes=[PSTK])
                S.op('act', lambda e, pc=pc: e.activation(convcT[:, pc * 4:pc * 4 + 4, :], PST[:, 0:128].rearrange("p (j n) -> p j n", j=4), AF.Copy),
                     reads=[PSTK], writes=convcTk)

        kk = [0]

        def conv_e(pt, pk, cc, cbuf, cbk, t, n):
            S.op('act', lambda e: e.activation(cbuf[:, 0:n], pt[:, 0:n], AF.Identity, bias=cb(cc), scale=cw(2, cc)),
                 reads=[pk, 'vec'], writes=cbk)
            if t < 3:
                hb_ = hts[t % 2][0][:, cc, :]
                S.op('act', lambda e: e.activation(ht1[:, cc:cc + 1], pt[:, n - 1:n], AF.Identity, scale=cw(1, cc)),
                     reads=[pk, 'vec'], writes=ht1k)
                S.op('act', lambda e: e.activation(hb_[:, 0:1], pt[:, n - 2:n - 1], AF.Identity, bias=ht1[:, cc:cc + 1], scale=cw(0, cc)),
                     reads=[pk, 'vec'] + ht1k, writes=htk[t % 2])
                S.op('act', lambda e: e.activation(hb_[:, 1:2], pt[:, n - 1:n], AF.Identity, scale=cw(0, cc)),
                     reads=[pk, 'vec'], writes=htk[t % 2])
            elif t == 3:
                S.op('act', lambda e: e.activation(haloP[:, cc, :], pt[:, n - 2:n], AF.Copy), reads=[pk], writes=haloPk)
            return (pt, pk)

        def conv_f(ppk, cc, cbuf, cbk, t, n):
            pt, pk = ppk
            if t < 4:
                S.op('dve', lambda e: e.scalar_tensor_tensor(cbuf[:, 1:n], pt[:, 0:n - 1], cw(1, cc), cbuf[:, 1:n], ALU.mult, ALU.add),
                     reads=[pk, 'vec'] + cbk, writes=cbk)
                S.op('dve', lambda e: e.scalar_tensor_tensor(cbuf[:, 2:n], pt[:, 0:n - 2], cw(0, cc), cbuf[:, 2:n], ALU.mult, ALU.add),
                     reads=[pk, 'vec'] + cbk, writes=cbk)
                if t > 0:
                    S.op('dve', lambda e: e.tensor_tensor(cbuf[:, 0:2], cbuf[:, 0:2], hts[(t - 1) % 2][0][:, cc, :], ALU.add),
                         reads=htk[(t - 1) % 2] + cbk, writes=cbk)
            else:
                c3 = cbuf[:, 0:128].rearrange("p (b t) -> p b t", t=LS)
                p3 = pt[:, 0:128].rearrange("p (b t) -> p b t", t=LS)
                cc3 = convcT[:, cc, :].rearrange("p (b t) -> p b t", t=2)
                S.op('dve', lambda e: e.scalar_tensor_tensor(c3[:, :, 1:8], p3[:, :, 0:7], cw(1, cc), c3[:, :, 1:8], ALU.mult, ALU.add),
                     reads=[pk, 'vec'] + cbk, writes=cbk)
                S.op('dve', lambda e: e.scalar_tensor_tensor(c3[:, :, 2:8], p3[:, :, 0:6], cw(0, cc), c3[:, :, 2:8], ALU.mult, ALU.add),
                     reads=[pk, 'vec'] + cbk, writes=cbk)
                S.op('dve', lambda e: e.scalar_tensor_tensor(c3[:, :, 0:1], cc3[:, :, 1:2], cw(1, cc), c3[:, :, 0:1], ALU.mult, ALU.add),
                     reads=convcTk + ['vec'] + cbk, writes=cbk)
                S.op('dve', lambda e: e.scalar_tensor_tensor(c3[:, :, 0:2], cc3[:, :, 0:2], cw(0, cc), c3[:, :, 0:2], ALU.mult, ALU.add),
                     reads=convcTk + ['vec'] + cbk, writes=cbk)
                S.op('dve', lambda e: e.tensor_copy(cc3, p3[:, :, 6:8]), reads=[pk], writes=convcTk)

        groups = [(0, 2), (2, 4), (6, 4), (10, 4), (14, 4), (18, 4)]
        items = [(gi, t) for gi in range(len(groups)) for t in range(len(TILES))]
        Wg = {}

        def loadw(gi):
            a0, na = groups[gi]
            wa = wload2(KC, 512, [(lambda v: v[:, :, 0:na * 128], kpn(wup_d[li][:, a0 * 128:(a0 + na) * 128]))])
            wg_ = wload2(KC, 512, [(lambda v: v[:, :, 0:na * 128], kpn(wup_d[li][:, DFF + a0 * 128:DFF + (a0 + na) * 128]))])
            wd = wload2(4, 1024, [(lambda v: v[:, 0:na, :], kpn(wdn_d[li][a0 * 128:(a0 + na) * 128, :]))])
            Wg[gi] = (wa, wg_, wd)

        def stA(i):
            gi, t = items[i]
            a0, na = groups[gi]
            t0, n = TILES[t]
            (wa, wak), (wg_, wgk), _ = Wg[gi]
            mt, mtk = mTs[i % 2]
            pend = {}

            def E(j):
                lst = []
                for (w, wkey, cc, (cbuf, cbk)) in ((wa, wak, a0 + j, cAs[j % 2]), (wg_, wgk, NFC + a0 + j, cGs[j % 2])):
                    pt, pk = ps_next()
                    S.group('pe', [mm(pt[:, 0:n], w[:, kc, j * 128:(j + 1) * 128], hT[:, kc, t0:t0 + n], kc == 0, kc == KC - 1)
                                   for kc in range(KC)], reads=[hk(t), wkey], writes=[pk])
                    i4 = conv_e(pt, pk, cc, cbuf, cbk, t, n)
                    lst.append((i4, cc, cbuf, cbk))
                pend[j] = lst

            def F(j):
                for (i4, cc, cbuf, cbk) in pend[j]:
                    conv_f(i4, cc, cbuf, cbk, t, n)
                cA, cAk = cAs[j % 2]
                cG, cGk = cGs[j % 2]
                sgt, sgtk = sgs[j % 2]
                S.op('act', lambda e: e.activation(sgt[:, 0:n], cG[:, 0:n], AF.Silu), reads=cGk, writes=sgtk)
                S.op('pool', lambda e: e.tensor_tensor(mt[:, j, 0:n], cA[:, 0:n], sgt[:, 0:n], ALU.mult), reads=cAk + sgtk, writes=mtk)

            E(0)
            for j in range(na):
                if j + 1 < na:
                    E(j + 1)
                F(j)

        def stB(i):
            gi, t = items[i]
            a0, na = groups[gi]
            t0, n = TILES[t]
            _, _, (wd, wdk) = Wg[gi]
            mt, mtk = mTs[i % 2]
            for dc in range(KC):
                pt, pk = ps_next()
                S.group('pe', [mm(pt[:, 0:n], wd[:, j, dc * 128:(dc + 1) * 128], mt[:, j, 0:n], j == 0, j == na - 1) for j in range(na)],
                        reads=[wdk] + mtk, writes=[pk])
                resid_add(pt, pk, dc, t, t0, n)

        loadw(0)
        loadw(1)
        stA(0)
        for i in range(len(items)):
            if i == 2:
                cache_prep()
            if i + 1 < len(items):
                stA(i + 1)
            stB(i)
            gi, t = items[i]
            if t == len(TILES) - 1 and gi + 2 < len(groups):
                loadw(gi + 2)
            if nn is not None and gi == len(groups) - 1:
                hoist_norm(nn, t, ntmp)
        for pc in range(11):
            stg, stgk = stgs[pc % 4]
            ptt, ptk = ps_next()
            S.group('pe', [tr(ptt[0:2, j * 128:(j + 1) * 128], haloP[:, pc * 4 + j, :], ident) for j in range(4)],
                    reads=haloPk + ['cst'], writes=[ptk])
            S.op('act', lambda e: e.activation(stg[0:2, :], ptt[0:2, :], AF.Copy), reads=[ptk], writes=stgk)
            S.dma('sp', cbp_d[li][:, pc * 512:(pc + 1) * 512], stg[0:2, :], reads=stgk)
        for pc in range(11):
            stg, stgk = stgs[pc % 4]
            ptt, ptk = ps_next()
            S.group('pe', [tr(ptt[0:32, j * 128:(j + 1) * 128], convcT[:, pc * 4 + j, :], ident) for j in range(4)],
                    reads=convcTk + ['cst'], writes=[ptk])
            S.op('act', lambda e: e.activation(stg[0:32, :], ptt[0:32, :], AF.Copy), reads=[ptk], writes=stgk)
            S.dma('sp', cbs_d[li][:, pc * 512:(pc + 1) * 512], stg[0:32, :], reads=stgk)

    def pool_mixer(nn=None):
        AR.reset()
        tmp = norm_tmp(512, 1)
        gcol = V_MIX + 8
        wp, wpk = wload2(KC, 256, [(ALL, pw_d.rearrange("g (c p) d -> p (g c) d", p=128))])
        poolcT, poolcTk = AR.alloc('poolcT', [128, KC, NS * 15], F32)
        haloHe = {'dve': AR.alloc('haloHe', [128, 4, 15], F32), 'pool': AR.alloc('haloHo', [128, 4, 15], F32)}
        hbs = {e: AR.alloc('hb', [128, 15 + 512], F32) for e in ('dve', 'pool')}
        sas = {e: [AR.alloc('sa', [128, 15 + 512], F32) for _ in range(2)] for e in ('dve', 'pool')}
        pT, pTk = AR.alloc('pT', [128, KC, 512], BF16)
        hS, hSk = AR.alloc('hS', [128, KC, 128], F32)
        stgo, stgok = AR.alloc('stgo', [128, D], F32)
        icn, icnk = AR.alloc('icn', [128, 4, 16], F32)
        tmp2 = dict(tmp)
        tmp2['rs'] = [AR.alloc('rstd2', [128, 512], F32)]
        tmp2['i'] = 0
        S.dma('sp', icn[:], icn_d.rearrange("p (g n) -> p g n", g=4)[:, :, 0:16], writes=icnk)
        for pc in range(2):
            S.dma('sp', stgo[0:120, :], cpool_d[pc * 120:(pc + 1) * 120, :], writes=stgok)
            for half in range(2):
                S.group('pe', [tr(PST[:, j * 120:(j + 1) * 120], stgo[0:120, (half * 4 + j) * 128:(half * 4 + j + 1) * 128], ident[0:120, 0:120])
                               for j in range(4)], reads=stgok + ['cst'], writes=[PSTK])
                S.op('act', lambda e, pc=pc, half=half: e.activation(
                    poolcT[:, half * 4:half * 4 + 4, pc * 120:(pc + 1) * 120], PST[:, 0:480].rearrange("p (j n) -> p j n", j=4), AF.Copy),
                    reads=[PSTK], writes=poolcTk)
        stats = {0: norm_stats(xT, [xk(0)], TILES[0][0], TILES[0][1], tmp)}
        for t, (t0, n) in enumerate(TILES):
            rs, rsk = stats[t]
            def chain(kc):
                g = kc // 2
                w = 2 << g
                EA = 'dve' if kc % 2 == 0 else 'pool'
                E_ = 'dve'
                hb, hbk = hbs[EA]
                haloHx, hHk = haloHe[EA]
                pTkk = pTk[kc * 2:kc * 2 + 2]
                gc_ = vec[:, gcol + kc:gcol + kc + 1]

                def mulmul(out, in0, okeys):
                    if E_ == 'dve':
                        S.op('dve', lambda e: e.scalar_tensor_tensor(out, in0, gc_, rs[:, 0:n], ALU.mult, ALU.mult),
                             reads=[xk(t), 'vec'] + rsk, writes=okeys)
                    else:
                        S.op('pool', lambda e: e.tensor_scalar(out, in0, gc_, None, ALU.mult), reads=[xk(t), 'vec'], writes=okeys)
                        S.op('pool', lambda e: e.tensor_tensor(out, out, rs[:, 0:n], ALU.mult), reads=rsk + okeys, writes=okeys)

                def scalesub(out, cu, cuk, hh, okeys):
                    if E_ == 'dve':
                        S.op('dve', lambda e: e.scalar_tensor_tensor(out, cu, 1.0 / w, hh, ALU.mult, ALU.subtract), reads=cuk + hbk, writes=okeys)
                    else:
                        S.op('pool', lambda e: e.tensor_scalar(cu, cu, 1.0 / w, None, ALU.mult), reads=cuk, writes=cuk)
                        S.op('pool', lambda e: e.tensor_tensor(out, cu, hh, ALU.subtract), reads=cuk + hbk, writes=okeys)

                if t < 4:
                    L = 15 + n
                    mulmul(hb[:, 15:L], xT[:, kc, t0:t0 + n], hbk)
                    if t == 0:
                        S.op(E_, lambda e: e.memset(hb[:, 0:15], 0.0), writes=hbk)
                    else:
                        S.op(E_, lambda e: e.tensor_copy(hb[:, 0:15], haloHx[:, kc // 2, :]), reads=hHk, writes=hbk)
                    S.op(E_, lambda e: e.tensor_copy(haloHx[:, kc // 2, :], hb[:, n:n + 15]), reads=hbk, writes=hHk)
                    cur, curk = hb, hbk
                    yield
                    for step in range(g + 1):
                        sh = 1 << step
                        v0 = (2 << step) - 1
                        nx, nxk = sas[EA][step % 2]
                        S.op(EA, lambda e, cur=cur, nx=nx: e.tensor_tensor(nx[:, v0:L], cur[:, v0:L], cur[:, v0 - sh:L - sh], ALU.add),
                             reads=curk, writes=nxk)
                        cur, curk = nx, nxk
                    yield
                    if t == 0:
                        S.op(E_, lambda e: e.tensor_tensor(cur[:, 15:31], cur[:, 15:31], icn[:, g, :], ALU.mult), reads=curk + icnk, writes=curk)
                        S.op(E_, lambda e: e.tensor_tensor(pT[:, kc, 0:16], cur[:, 15:31], hb[:, 15:31], ALU.subtract), reads=curk + hbk, writes=pTkk)
                        scalesub(pT[:, kc, 16:n], cur[:, 31:L], curk, hb[:, 31:L], pTkk)
                    else:
                        scalesub(pT[:, kc, 0:n], cur[:, 15:L], curk, hb[:, 15:L], pTkk)
                else:
                    v3 = lambda a, lo, hi: a[:, 0:NS * 23].rearrange("p (b x) -> p b x", x=23)[:, :, lo:hi]
                    hSkk = hSk[kc:kc + 1]
                    mulmul(hS[:, kc, :], xT[:, kc, t0:t0 + n], hSkk)
                    S.op(E_, lambda e: e.tensor_copy(v3(hb, 15, 23), hS[:, kc, :].rearrange("p (b t) -> p b t", t=LS)), reads=hSkk, writes=hbk)
                    S.op(E_, lambda e: e.tensor_copy(v3(hb, 0, 15), poolcT[:, kc, :].rearrange("p (b t) -> p b t", t=15)),
                         reads=poolcTk, writes=hbk)
                    cur, curk = hb, hbk
                    yield
                    for step in range(g + 1):
                        sh = 1 << step
                        v0 = (2 << step) - 1
                        nx, nxk = sas[EA][step % 2]
                        S.op(EA, lambda e, cur=cur, nx=nx: e.tensor_tensor(v3(nx, v0, 23), v3(cur, v0, 23), v3(cur, v0 - sh, 23 - sh), ALU.add),
                             reads=curk, writes=nxk)
                        cur, curk = nx, nxk
                    yield
                    scalesub(pT[:, kc, 0:128].rearrange("p (b t) -> p b t", t=LS), v3(cur, 15, 23), curk, v3(hb, 15, 23), pTkk)
            for p_ in range(KC // 2):
                ge, go = chain(2 * p_), chain(2 * p_ + 1)
                next(go)
                next(go)
                for _ in ge:
                    pass
                for _ in go:
                    pass
            if t + 1 < len(TILES):
                stats[t + 1] = norm_stats(xT, [xk(t + 1)], TILES[t + 1][0], TILES[t + 1][1], tmp)
            haloHk = haloHe['dve'][1] + haloHe['pool'][1]
            for g in range(4):
                for dd in range(2):
                    pt, pk = ps_next()
                    S.group('pe', [mm(pt[:, 0:n], wp[:, g * 2 + c, dd * 128:(dd + 1) * 128], pT[:, 2 * g + c, 0:n], c == 0, c == 1)
                                   for c in range(2)], reads=[wpk] + pTk, writes=[pk])
                    resid_add(pt, pk, 2 * g + dd, t, t0, n, scol=vec[:, V_PSC + 2 * g + dd:V_PSC + 2 * g + dd + 1])
            if nn is not None:
                hoist_norm(nn, t, tmp2)
            if t == 3:
                for half in range(2):
                    S.group('pe', [tr(PST[0:15, j * 128:(j + 1) * 128],
                                      haloHe['dve' if (half * 4 + j) % 2 == 0 else 'pool'][0][:, (half * 4 + j) // 2, :], ident) for j in range(4)],
                            reads=haloHk + ['cst'], writes=[PSTK])
                    S.op('act', lambda e, half=half: e.activation(stgo[0:15, half * 512:(half + 1) * 512], PST[0:15, :], AF.Copy),
                         reads=[PSTK], writes=stgok)
                S.dma('sp', pbp_d, stgo[0:15, :], reads=stgok)
        S.dma('sp', pbs_d[:, 0:7, :], cpool_d.rearrange("(b r) f -> b r f", r=15)[:, 8:15, :])
        for half in range(2):
            S.group('pe', [tr(PST[:, j * 128:(j + 1) * 128], hS[:, half * 4 + j, :], ident) for j in range(4)],
                    reads=hSk + ['cst'], writes=[PSTK])
            S.op('act', lambda e, half=half: e.activation(stgo[:, half * 512:(half + 1) * 512], PST[:, :], AF.Copy), reads=[PSTK], writes=stgok)
        for b in range(NS):
            S.dma('sp', pbs_d[b, 7:15, :], stgo[8 * b:8 * b + 8, :], reads=stgok)

    def pool_mixer_pe(nn=None):
        AR.reset()
        gcol = V_MIX + 8
        wp, wpk = wload2(KC, 256, [(ALL, pw_d.rearrange("g (c p) d -> p (g c) d", p=128))])
        pm, pmk = AR.alloc('pm', [128, 24, 128], BF16)
        S.dma('pool', pm[:].rearrange("p a b -> p (a b)"), pmat_d, writes=pmk)
        PM = lambda g, k: pm[:, g * 6 + k, :]
        tmpS = norm_tmp(128, 1)
        tmp2 = norm_tmp(512, 1) if nn is not None else None
        poolcT, poolcTk = AR.alloc('poolcT', [128, KC, NS * 15], BF16)
        hS, hSk = AR.alloc('hS', [128, KC, 128], F32)
        hP, hPk = AR.alloc('hP', [128, KC, 16], F32)
        stgo, stgok = AR.alloc('stgo', [128, D], F32)
        stgc = [AR.alloc('stgc', [128, D], F32)] * 2
        zs = [AR.alloc('z', [128, D], BF16) for _ in range(5)]
        zc = [AR.alloc('zc', [128, D], BF16) for _ in range(2)]
        for pc in range(2):
            S.dma('sp', stgc[pc][0][0:120, :], cpool_d[pc * 120:(pc + 1) * 120, :], writes=stgc[pc][1])
            for half in range(2):
                ptt, ptk = ps_next()
                S.group('pe', [tr(ptt[:, j * 120:(j + 1) * 120], stgc[pc][0][0:120, (half * 4 + j) * 128:(half * 4 + j + 1) * 128], ident[0:120, 0:120])
                               for j in range(4)], reads=stgc[pc][1] + ['cst'], writes=[ptk])
                S.op('act', lambda e, pc=pc, half=half, ptt=ptt: e.activation(
                    poolcT[:, half * 4:half * 4 + 4, pc * 120:(pc + 1) * 120], ptt[:, 0:480].rearrange("p (j n) -> p j n", j=4), AF.Copy),
                    reads=[ptk], writes=poolcTk)

        rs, rsk = norm_stats(xT, [xk(3)], SEQ - 16, 16, tmpS)
        for kc in range(KC):
            S.op('dve', lambda e, kc=kc: e.scalar_tensor_tensor(hP[:, kc, :], xT[:, kc, SEQ - 16:SEQ], vec[:, gcol + kc:gcol + kc + 1], rs[:, 0:16],
                                                             ALU.mult, ALU.mult), reads=[xk(3), 'vec'] + rsk, writes=hPk)
        rs, rsk = norm_stats(xT, [xk(4)], SEQ, 128, tmpS)
        for kc in range(KC):
            S.op('dve', lambda e, kc=kc: e.scalar_tensor_tensor(hS[:, kc, :], xT[:, kc, SEQ:SEQ + 128], vec[:, gcol + kc:gcol + kc + 1], rs[:, 0:128],
                                                             ALU.mult, ALU.mult), reads=[xk(4), 'vec'] + rsk, writes=hSk)
        for half in range(2):
            ptt, ptk = ps_next()
            S.group('pe', [tr(ptt[0:15, j * 128:(j + 1) * 128], hP[:, half * 4 + j, 1:16], ident) for j in range(4)],
                    reads=hPk + ['cst'], writes=[ptk])
            S.op('act', lambda e, half=half, ptt=ptt: e.activation(stgo[0:15, half * 512:(half + 1) * 512], ptt[0:15, :], AF.Copy),
                 reads=[ptk], writes=stgok)
        S.dma('sp', pbp_d, stgo[0:15, :], reads=stgok)
        S.dma('sp', pbs_d[:, 0:7, :], cpool_d.rearrange("(b r) f -> b r f", r=15)[:, 8:15, :])
        for half in range(2):
            ptt, ptk = ps_next()
            S.group('pe', [tr(ptt[:, j * 128:(j + 1) * 128], hS[:, half * 4 + j, :], ident) for j in range(4)],
                    reads=hSk + ['cst'], writes=[ptk])
            S.op('act', lambda e, half=half, ptt=ptt: e.activation(stgo[:, half * 512:(half + 1) * 512], ptt[:, :], AF.Copy), reads=[ptk], writes=stgok)
        for b in range(NS):
            S.dma('sp', pbs_d[b, 7:15, :], stgo[8 * b:8 * b + 8, :], reads=stgok)
        def zproj(lhs_fn, rows, zt, ztk, rkeys):
            for hf in range(2):
                ptt, ptk = ps_next()
                for gg in range(2):
                    g = hf * 2 + gg
                    S.group('pe', [mm(ptt[0:rows, gg * 256:(gg + 1) * 256], lhs_fn(2 * g + cc), wp[:, g * 2 + cc, :], cc == 0, cc == 1)
                                   for cc in range(2)], reads=rkeys + [wpk], writes=[ptk])
                S.op('act', lambda e, hf=hf, ptt=ptt: e.activation(zt[0:rows, hf * 512:(hf + 1) * 512], ptt[0:rows, :], AF.Copy),
                     reads=[ptk], writes=ztk)

        for pc in range(2):
            zproj(lambda kc, pc=pc: poolcT[:, kc, pc * 120:(pc + 1) * 120], 120, zc[pc][0], zc[pc][1], poolcTk)
        zi = 0
        prev = None
        for t, (t0, n) in enumerate(TILES[:4]):
            cur = []
            for sub in range(4):
                zt, ztk = zs[zi % 5]
                zi += 1
                c0 = t0 + sub * 128
                zproj(lambda kc, c0=c0: hT[:, kc, c0:c0 + 128], 128, zt, ztk, [hk(t)])
                cur.append((zt, ztk))
            for g in range(4):
                for dd in range(2):
                    fs = slice(g * 256 + dd * 128, g * 256 + (dd + 1) * 128)
                    pt, pk = ps_next()
                    for sub in range(4):
                        zt, ztk = cur[sub]
                        first = (t == 0 and sub == 0)
                        pz = prev if sub == 0 else cur[sub - 1]
                        fns = [mm(pt[:, sub * 128:(sub + 1) * 128], zt[:, fs], PM(g, 2 if first else 0), True, first)]
                        rd = ztk + pmk
                        if not first:
                            fns.append(mm(pt[:, sub * 128:(sub + 1) * 128], pz[0][:, fs], PM(g, 1), False, True))
                            rd = rd + pz[1]
                        S.group('pe', fns, reads=rd, writes=[pk])
                    resid_add(pt, pk, 2 * g + dd, t, t0, n, scol=vec[:, V_PSC + 2 * g + dd:V_PSC + 2 * g + dd + 1])
            prev = cur[3]
            if nn is not None:
                hoist_norm(nn, t, tmp2)
        t, (t0, n) = 4, TILES[4]
        zt, ztk = zs[zi % 5]
        zproj(lambda kc: hT[:, kc, t0:t0 + 128], 128, zt, ztk, [hk(t)])
        for g in range(4):
            for dd in range(2):
                fs = slice(g * 256 + dd * 128, g * 256 + (dd + 1) * 128)
                pt, pk = ps_next()
                S.group('pe', [mm(pt[:, 0:128], zt[:, fs], PM(g, 3), True, False),
                               mm(pt[:, 0:128], zc[0][0][0:120, fs], pm[0:120, g * 6 + 4, :], False, False),
                               mm(pt[:, 0:128], zc[1][0][0:120, fs], pm[0:120, g * 6 + 5, :], False, True)],
                        reads=ztk + zc[0][1] + zc[1][1] + pmk, writes=[pk])
                resid_add(pt, pk, 2 * g + dd, t, t0, n, scol=vec[:, V_PSC + 2 * g + dd:V_PSC + 2 * g + dd + 1])
        if nn is not None:
            hoist_norm(nn, t, tmp2)

    def final():
        AR.reset()
        tmp = norm_tmp(512, 2)
        yTs = [AR.alloc('yT', [128, KC, 128], F32) for _ in range(2)]
        yos = [AR.alloc('yo', [128, D], F32) for _ in range(3)]
        stats = {0: norm_stats(xT, [xk(0)], TILES[0][0], TILES[0][1], tmp)}
        i = 0
        for t, (t0, n) in enumerate(TILES):
            if t + 1 < len(TILES):
                stats[t + 1] = norm_stats(xT, [xk(t + 1)], TILES[t + 1][0], TILES[t + 1][1], tmp)
            rs, rsk = stats[t]
            for sub in range(n // 128):
                c0 = t0 + sub * 128
                yT, yTk = yTs[i % 2]
                yo, yok = yos[i % 3]
                i += 1
                for kc in range(KC):
                    S.op('dve', lambda e, kc=kc: e.scalar_tensor_tensor(
                        yT[:, kc, :], xT[:, kc, c0:c0 + 128], vec[:, V_FIN + kc:V_FIN + kc + 1], rs[:, sub * 128:(sub + 1) * 128],
                        ALU.mult, ALU.mult), reads=[xk(t), 'vec'] + rsk, writes=yTk)
                for half in range(2):
                    ptt, ptk = ps_next()
                    S.group('pe', [tr(ptt[:, j * 128:(j + 1) * 128], yT[:, half * 4 + j, :], ident) for j in range(4)],
                            reads=yTk + ['cst'], writes=[ptk])
                    S.op('act', lambda e, half=half, ptt=ptt: e.activation(yo[:, half * 512:(half + 1) * 512], ptt[:, :], AF.Copy), reads=[ptk], writes=yok)
                S.dma('sp', y_d[c0:c0 + 128, :] if t < 4 else ys_d, yo[:], reads=yok)

    import os
    PH = os.environ.get("KPH", "all")
    io_in(nn=V_MIX)
    if PH in ("all", "ret", "l0"):
        for h in range(H):
            retention_head(h, nn=(V_XA if h == H - 1 else None))
    if PH in ("all", "l0"):
        attention(0, pre_norm=False, nn=V_FFN)
        ffn(0, pre_norm=False, nn=V_MIX + 8)
    if PH in ("all",):
        pool_mixer_pe(nn=V_XA + 8)
        attention(1, pre_norm=False, nn=V_FFN + 8)
        ffn(1, pre_norm=False)
        final()
    if PH == "mem":
        memkv(0)
        memkv(1)
    S.finish()
    print("instructions:", S.nins, "waits:", S.nwait, "sems:", S.nsem, flush=True)
    return nc


def _host_consts():
    c = {}
    cst = np.zeros((128, 128 + 16 + 2 + 248 + 256), np.float32)
    cst[:, 0:128] = np.eye(128, dtype=np.float32)
    for b in range(NS):
        cst[b * 8:(b + 1) * 8, 128 + b] = 1.0
    cst[:, 144] = EPS
    cst[:, 145] = GN_EPS
    cst[:, 146 + 120:146 + 128] = 1.0
    cst[:, 394:522] = 1.0 / D
    cst[:, 522:650] = 1.0
    c['cst'] = cst
    inv = 1.0 / (10000.0 ** (np.arange(0, DK, 2, dtype=np.float64) / float(DK)))
    pos = np.concatenate([np.arange(SEQ, dtype=np.float64),
                          np.tile(float(PAST) + np.arange(LS, dtype=np.float64), NS)])
    ang = pos[:, None] * inv[None, :]
    c['cosT'] = np.ascontiguousarray(np.cos(ang).T.astype(np.float32))
    c['sinT'] = np.ascontiguousarray(np.sin(ang).T.astype(np.float32))
    lg = np.log(1.0 - 2.0 ** (-5.0 - np.arange(H, dtype=np.float64)))
    hcf = np.zeros((H, 128, 258), np.float32)
    hcq = np.zeros((H, 128, 640), np.float32)
    i128 = np.arange(128)
    for h in range(H):
        k = i128[:, None]
        q = i128[None, :]
        mp = np.where(q >= k, np.exp((q - k) * lg[h]), 0.0) * (DK ** -0.5)
        ms = np.where((q >= k) & (q // LS == k // LS), np.exp((q - k) * lg[h]), 0.0) * (DK ** -0.5)
        hcf[h, :, 0:128] = mp
        hcf[h, :, 128:256] = ms
        hcf[h, :, 256] = np.exp((127 - i128) * lg[h]) * (DK ** -0.5)
        hcf[h, :, 257] = np.exp((LS - 1 - (i128 % LS)) * lg[h]) * (DK ** -0.5)
        hcq[h, :, 0:512] = np.exp(((np.arange(512) % 128) + 1) * lg[h])[None, :]
        hcq[h, :, 512:640] = np.exp(((np.arange(128) % LS) + 1) * lg[h])[None, :]
    c['hcf'] = hcf
    c['hcq'] = hcq
    icn = np.zeros((128, 4 * 512), np.float32)
    for g, w in enumerate((2, 4, 8, 16)):
        icn[:, g * 512:(g + 1) * 512] = (1.0 / np.minimum(np.arange(512) + 1.0, float(w)))[None, :]
    c['icn'] = icn
    pmat = np.zeros((128, 24, 128), np.float32)
    i128 = np.arange(128)
    for g, w in enumerate((2, 4, 8, 16)):
        kp = i128[:, None]
        p = i128[None, :]
        win = ((kp <= p) & (kp > p - w)).astype(np.float64)
        eye = (kp == p).astype(np.float64)
        pmat[:, g * 6 + 0, :] = win / w - eye
        pmat[:, g * 6 + 1, :] = ((kp - 128 > p - w)).astype(np.float64) / w
        pmat[:, g * 6 + 2, :] = win / np.minimum(p + 1.0, float(w)) - eye
        sb = (kp // LS == p // LS)
        pmat[:, g * 6 + 3, :] = np.where(sb, win / w - eye, 0.0)
        for half in range(2):
            r = i128[:, None]
            bq = r // 15 + 8 * half
            r15 = r % 15
            t_ = p % LS
            ok = (r < 120) & (bq == p // LS) & (r15 > 15 + t_ - w)
            pmat[:, g * 6 + 4 + half, :] = np.where(ok, 1.0 / w, 0.0)
    c['pmat'] = pmat.reshape(128, 24 * 128)
    return c


def _cols(v):
    return np.ascontiguousarray(v.reshape(-1, 128).T)


_CACHE = {}


def kernel(x_prompt, x_sample, mem_prompt, cache_mem_k, cache_mem_v, state_ret, cache_pool, cache_ffn_conv,
           w_ret_in, ret_gn, w_ret_out, pool_w, pool_scale, norm_mem, w_xq, w_xk, w_xv, w_xo,
           w_up, conv_w, conv_b, w_down, norm_mix, norm_xattn, norm_ffn, norm_final):
    f = lambda a: np.ascontiguousarray(np.asarray(a, dtype=np.float32))
    x_prompt, x_sample, mem_prompt = f(x_prompt), f(x_sample), f(mem_prompt)
    cache_mem_k, cache_mem_v, state_ret = f(cache_mem_k), f(cache_mem_v), f(state_ret)
    cache_pool, cache_ffn_conv = f(cache_pool), f(cache_ffn_conv)
    if 'nc' not in _CACHE:
        _CACHE['nc'] = build_program()
        _CACHE['c'] = _host_consts()
    nc = _CACHE['nc']
    cc = _CACHE['c']
    vec = np.zeros((128, NVEC), np.float32)
    for i in range(2):
        vec[:, V_MIX + 8 * i:V_MIX + 8 * i + 8] = _cols(f(norm_mix)[i])
        vec[:, V_XA + 8 * i:V_XA + 8 * i + 8] = _cols(f(norm_xattn)[i])
        vec[:, V_FFN + 8 * i:V_FFN + 8 * i + 8] = _cols(f(norm_ffn)[i])
        vec[:, V_MEM + 8 * i:V_MEM + 8 * i + 8] = _cols(f(norm_mem)[i])
        for j in range(3):
            vec[:, V_CW + (i * 3 + j) * 44:V_CW + (i * 3 + j + 1) * 44] = _cols(f(conv_w)[i, j])
        vec[:, V_CB + i * 44:V_CB + (i + 1) * 44] = _cols(f(conv_b)[i])
    vec[:, V_FIN:V_FIN + 8] = _cols(f(norm_final))
    vec[:, V_PSC:V_PSC + 8] = _cols(f(pool_scale)[0])
    shared = {
        "w_in": f(w_ret_in)[0], "gn": f(ret_gn).reshape(1, 2048), "w_out": f(w_ret_out)[0], "pw": f(pool_w)[0],
        "wq": f(w_xq), "wk": f(w_xk), "wv": f(w_xv), "wo": f(w_xo), "wup": f(w_up), "wdn": f(w_down),
        "vec": vec, "cst": cc['cst'], "cosT": cc['cosT'], "sinT": cc['sinT'], "hcf": cc['hcf'], "hcq": cc['hcq'],
        "icn": cc['icn'], "pmat": cc['pmat'],
    }
    in_maps = []
    for c in range(NCORES):
        sl = slice(c * NS, (c + 1) * NS)
        m = dict(shared)
        m["x"] = x_prompt[c]
        m["xs"] = x_sample[sl].reshape(NS * LS, D)
        m["mem"] = mem_prompt[c]
        m["cmk"] = np.ascontiguousarray(cache_mem_k[:, sl].reshape(2, NS, NMEM, D))
        m["cmv"] = np.ascontiguousarray(cache_mem_v[:, sl].reshape(2, NS, NMEM, D))
        m["sret"] = np.ascontiguousarray(state_ret[0, sl])
        m["cpool"] = np.ascontiguousarray(cache_pool[0, sl].reshape(NS * 15, D))
        m["cconv"] = np.ascontiguousarray(cache_ffn_conv[:, sl].reshape(2, NS * 2, 2 * DFF))
        in_maps.append(m)
    res = run_bass_kernel_spmd(nc, in_maps, core_ids=list(range(NCORES)))
    R = res.results
    st = lambda k: np.stack([np.asarray(R[c][k], dtype=np.float32) for c in range(NCORES)])
    y_prompt = st("y")
    y_sample = st("ys").reshape(NCORES * NS, LS, D)
    ret_p = st("rsp")[None]
    ret_s = st("rss").reshape(NCORES * NS, H, DK, DV)[None]
    pool_p = st("pbp")[None]
    pool_s = st("pbs").reshape(NCORES * NS, 15, D)[None]
    conv_p = np.ascontiguousarray(st("cbp").transpose(1, 0, 2, 3))
    conv_s = np.ascontiguousarray(st("cbs").reshape(NCORES, 2, NS, 2, 2 * DFF).transpose(1, 0, 2, 3, 4)).reshape(2, NCORES * NS, 2, 2 * DFF)
    mk = np.ascontiguousarray(st("mkp").transpose(1, 0, 2, 3)).reshape(2, NCORES, NMEM, 4, 256)
    mv = np.ascontiguousarray(st("mvp").transpose(1, 0, 2, 3)).reshape(2, NCORES, NMEM, 4, 256)
    return (y_prompt, y_sample, ret_p, ret_s, pool_p, pool_s, conv_p, conv_s, mk, mv)
```
